# Optimizing a Trainium2 kernel written in Bass

```python
import math
import jax, jax.numpy as jnp
from jax import lax
import numpy as np

D_MODEL = 1024
BATCH = 2
SEQ = 16384
DEPTH = 2
DEC_BATCH = 8
DEC_SEQ = 64
PAST_LEN = 2048

CHUNK = 64
Q_BLOCK = 128
N_MIXERS = 4
D_MIX = D_MODEL
W_GROUP = D_MIX // N_MIXERS
MLA_HEADS = 4
MLA_V_DIM = W_GROUP // MLA_HEADS
MLA_NOPE_DIM = 64
MLA_ROPE_DIM = 32
MLA_QK_DIM = MLA_NOPE_DIM + MLA_ROPE_DIM
MLA_Q_LORA = D_MODEL // 4
MLA_KV_LORA = D_MODEL // 8
ROPE_THETA = 10000.0
SSM_WIDTH = W_GROUP
SSM_HEAD_DIM = 64
SSM_HEADS = SSM_WIDTH // SSM_HEAD_DIM
SSM_GROUPS = 2
SSM_STATE = 128
SSM_CONV = 4
SSM_XBC = SSM_WIDTH + 2 * SSM_GROUPS * SSM_STATE
LRU_WIDTH = W_GROUP
LRU_BLOCKS = 4
LRU_BLOCK_DIM = LRU_WIDTH // LRU_BLOCKS
LRU_CONV = 4
LRU_C = 8.0
SC_WIDTH = W_GROUP
SC_CONV = 3
D_FF = 2816
FFN_CONV = 3
EPS = 1e-6
IN_SIZES = (MLA_Q_LORA, MLA_KV_LORA, MLA_ROPE_DIM, SSM_WIDTH, SSM_XBC, SSM_HEADS,
            LRU_WIDTH, LRU_WIDTH, SC_WIDTH, SC_WIDTH, SC_WIDTH)
IN_WIDTH = (MLA_Q_LORA + MLA_KV_LORA + MLA_ROPE_DIM + SSM_WIDTH + SSM_XBC + SSM_HEADS
            + 2 * LRU_WIDTH + 3 * SC_WIDTH)

kernel_name = 'hybrid_stream_encoder_step'


def _rmsnorm(x, g):
    xf = x.astype(jnp.float32)
    y = xf * lax.rsqrt(jnp.mean(xf * xf, axis=-1, keepdims=True) + EPS)
    return (y * g.astype(jnp.float32)).astype(x.dtype)


def _rope(x, pos):
    half = x.shape[-1] // 2
    inv = ROPE_THETA ** (-jnp.arange(half, dtype=jnp.float32) / half)
    ang = (pos[:, None] * inv[None, :]).reshape((pos.shape[0],) + (1,) * (x.ndim - 3) + (half,))
    cos, sin = jnp.cos(ang), jnp.sin(ang)
    xf = x.astype(jnp.float32)
    x1, x2 = xf[..., :half], xf[..., half:]
    return jnp.concatenate([x1 * cos - x2 * sin, x1 * sin + x2 * cos], axis=-1).astype(x.dtype)


def _causal_conv(x, w, b, prev):
    K = w.shape[0]
    T = x.shape[1]
    xp = jnp.concatenate([prev.astype(x.dtype), x], axis=1)
    y = xp[:, 0:T] * w[0]
    for k in range(1, K):
        y = y + xp[:, k:k + T] * w[k]
    if b is not None:
        y = y + b
    return y, xp[:, T:]


def _attn_full(q, k, v):
    s = jnp.einsum('bqhd,bkhd->bhqk', q, k).astype(jnp.float32) * (MLA_QK_DIM ** -0.5)
    p = jax.nn.softmax(s, axis=-1).astype(v.dtype)
    return jnp.einsum('bhqk,bkhd->bqhd', p, v)


def _attn_chunk_causal(q, k, v):
    b, T, H, dq = q.shape
    nb = T // Q_BLOCK
    q_blocks = q.reshape(b, nb, Q_BLOCK, H, dq).transpose(1, 0, 2, 3, 4)
    key_chunk = jnp.arange(T) // CHUNK

    def one_block(args):
        qb, i = args
        q_chunk = (i * Q_BLOCK + jnp.arange(Q_BLOCK)) // CHUNK
        s = jnp.einsum('bqhd,bkhd->bhqk', qb, k).astype(jnp.float32) * (MLA_QK_DIM ** -0.5)
        mask = key_chunk[None, :] <= q_chunk[:, None]
        s = jnp.where(mask[None, None], s, -jnp.inf)
        p = jax.nn.softmax(s, axis=-1).astype(v.dtype)
        return jnp.einsum('bhqk,bkhd->bqhd', p, v)

    o = lax.map(one_block, (q_blocks, jnp.arange(nb)))
    return o.transpose(1, 0, 2, 3, 4).reshape(b, T, H, v.shape[-1])


def _mla(q_lat, kv_lat, kr_raw, pos, past_ckv, past_kr, p):
    b, T = q_lat.shape[:2]
    q = (_rmsnorm(q_lat, p['q_a_norm_g']) @ p['w_q_b']).reshape(b, T, MLA_HEADS, MLA_QK_DIM)
    q = jnp.concatenate([q[..., :MLA_NOPE_DIM], _rope(q[..., MLA_NOPE_DIM:], pos)], axis=-1)
    q = _rmsnorm(q, p['q_norm_g'])
    ckv = _rmsnorm(kv_lat, p['kv_a_norm_g'])
    kr = _rope(kr_raw, pos)
    if past_ckv is None:
        ckv_all, kr_all = ckv, kr
    else:
        ckv_all = jnp.concatenate([past_ckv.astype(ckv.dtype), ckv], axis=1)
        kr_all = jnp.concatenate([past_kr.astype(kr.dtype), kr], axis=1)
    Tk = ckv_all.shape[1]
    kv = (ckv_all @ p['w_kv_b']).reshape(b, Tk, MLA_HEADS, MLA_NOPE_DIM + MLA_V_DIM)
    k = jnp.concatenate([kv[..., :MLA_NOPE_DIM],
                         jnp.broadcast_to(kr_all[:, :, None, :], (b, Tk, MLA_HEADS, MLA_ROPE_DIM))], axis=-1)
    k = _rmsnorm(k, p['k_norm_g'])
    v = kv[..., MLA_NOPE_DIM:]
    if past_ckv is None:
        o = _attn_chunk_causal(q, k, v)
    else:
        o = _attn_full(q, k, v)
    return o.reshape(b, T, W_GROUP), ckv, kr


def _ssd_scan(x, dt, a, bm, cm, h0):
    b, T, H, P = x.shape
    N = bm.shape[-1]
    L = min(CHUNK, T)
    nc = T // L
    f32 = jnp.float32
    x = x.astype(f32).reshape(b, nc, L, H, P)
    dt = dt.reshape(b, nc, L, H)
    bm = bm.astype(f32).reshape(b, nc, L, H, N)
    cm = cm.astype(f32).reshape(b, nc, L, H, N)
    cum = jnp.cumsum(dt * a, axis=2)
    causal = jnp.tril(jnp.ones((L, L), dtype=bool))[None, None, :, :, None]
    seg = cum[:, :, :, None, :] - cum[:, :, None, :, :]
    decay = jnp.where(causal, jnp.exp(jnp.where(causal, seg, 0.0)), 0.0)
    scores = jnp.einsum('bcthn,bcshn->bctsh', cm, bm) * decay * dt[:, :, None, :, :]
    y_diag = jnp.einsum('bctsh,bcshp->bcthp', scores, x)
    w_end = jnp.exp(cum[:, :, -1:, :] - cum) * dt
    chunk_states = jnp.einsum('bcsh,bcshn,bcshp->bchpn', w_end, bm, x)
    chunk_decay = jnp.exp(cum[:, :, -1, :])

    def step(h, inp):
        s_c, d_c = inp
        return d_c[:, :, None, None] * h + s_c, h

    h_final, h_in = lax.scan(step, h0.astype(f32),
                             (chunk_states.transpose(1, 0, 2, 3, 4), chunk_decay.transpose(1, 0, 2)))
    h_in = h_in.transpose(1, 0, 2, 3, 4)
    y_off = jnp.einsum('bcthn,bchpn->bcthp', cm, h_in) * jnp.exp(cum)[..., None]
    return (y_diag + y_off).reshape(b, T, H, P), h_final


def _ssd_mixer(z, xbc, dt_raw, conv0, h0, p):
    b, T, _ = z.shape
    xbc, conv1 = _causal_conv(xbc, p['ssm_conv_w'], p['ssm_conv_b'], conv0)
    xbc = jax.nn.silu(xbc)
    xs, bm, cm = jnp.split(xbc, [SSM_WIDTH, SSM_WIDTH + SSM_GROUPS * SSM_STATE], axis=-1)
    rep = SSM_HEADS // SSM_GROUPS
    xs = xs.reshape(b, T, SSM_HEADS, SSM_HEAD_DIM)
    bm = jnp.repeat(bm.reshape(b, T, SSM_GROUPS, SSM_STATE), rep, axis=2)
    cm = jnp.repeat(cm.reshape(b, T, SSM_GROUPS, SSM_STATE), rep, axis=2)
    dt = jax.nn.softplus(dt_raw.astype(jnp.float32) + p['ssm_dt_bias'].astype(jnp.float32))
    a = -jnp.exp(p['ssm_a_log'].astype(jnp.float32))
    y, h1 = _ssd_scan(xs, dt, a, bm, cm, h0)
    y = y + p['ssm_d'].astype(jnp.float32)[:, None] * xs.astype(jnp.float32)
    y = y.reshape(b, T, SSM_WIDTH) * jax.nn.silu(z.astype(jnp.float32))
    return y.astype(z.dtype), conv1, h1


def _rglru_mixer(xb, gate, conv0, h0, p):
    b, T, _ = xb.shape
    xc, conv1 = _causal_conv(xb, p['lru_conv_w'], p['lru_conv_b'], conv0)
    xblk = xc.reshape(b, T, LRU_BLOCKS, LRU_BLOCK_DIM)
    r = jax.nn.sigmoid((jnp.einsum('btki,kij->btkj', xblk, p['lru_w_a']).reshape(b, T, LRU_WIDTH)
                        + p['lru_b_a']).astype(jnp.float32))
    i = jax.nn.sigmoid((jnp.einsum('btki,kij->btkj', xblk, p['lru_w_x']).reshape(b, T, LRU_WIDTH)
                        + p['lru_b_x']).astype(jnp.float32))
    log_a = -LRU_C * r * jax.nn.softplus(-p['lru_lambda'].astype(jnp.float32))
    a = jnp.exp(log_a)
    u = jnp.sqrt(-jnp.expm1(2.0 * log_a)) * (i * xc.astype(jnp.float32))
    u = u.at[:, 0].add(a[:, 0] * h0.astype(jnp.float32))

    def comb(left, right):
        a1, b1 = left
        a2, b2 = right
        return a1 * a2, a2 * b1 + b2

    _, h = lax.associative_scan(comb, (a, u), axis=1)
    y = h * jax.nn.gelu(gate.astype(jnp.float32))
    return y.astype(xb.dtype), conv1, h[:, -1]


def _short_conv_mixer(bg, cg, hin, conv0, w):
    v, conv1 = _causal_conv(cg * hin, w, None, conv0)
    return bg * v, conv1


def _conv_ffn(h, conv0, p):
    u = h @ p['w_ffn_up']
    u, conv1 = _causal_conv(u, p['ffn_conv_w'], p['ffn_conv_b'], conv0)
    g, v = jnp.split(u, 2, axis=-1)
    return (jax.nn.silu(g) * v) @ p['w_ffn_down'], conv1


def _layer(x, pos0, past_ckv, past_kr, ssm_conv0, ssm0, lru_conv0, lru0, sc_conv0, ffn_conv0, p):
    b, T, _ = x.shape
    pos = jnp.arange(T, dtype=jnp.float32) + jnp.float32(pos0)
    proj = _rmsnorm(x, p['ln_mix_g']) @ p['w_in']
    cuts = np.cumsum(IN_SIZES)[:-1].tolist()
    (q_lat, kv_lat, kr_raw, z, xbc, dt_raw, lru_x, lru_g, sc_b, sc_c, sc_h) = jnp.split(proj, cuts, axis=-1)
    o_a, ckv, kr = _mla(q_lat, kv_lat, kr_raw, pos, past_ckv, past_kr, p)
    o_b, ssm_conv1, ssm1 = _ssd_mixer(z, xbc, dt_raw, ssm_conv0, ssm0, p)
    o_c, lru_conv1, lru1 = _rglru_mixer(lru_x, lru_g, lru_conv0, lru0, p)
    o_d, sc_conv1 = _short_conv_mixer(sc_b, sc_c, sc_h, sc_conv0, p['sc_conv_w'])
    mix = jnp.concatenate([o_a, o_b, o_c, o_d], axis=-1).reshape(b, T, N_MIXERS, W_GROUP)
    mix = _rmsnorm(mix, p['out_norm_g'].reshape(N_MIXERS, W_GROUP)).reshape(b, T, D_MIX)
    x = x + mix @ p['w_out']
    f, ffn_conv1 = _conv_ffn(_rmsnorm(x, p['ln_ffn_g']), ffn_conv0, p)
    x = x + f
    return x, (ckv, kr, ssm_conv1, ssm1, lru_conv1, lru1, sc_conv1, ffn_conv1)


def setup_inputs(seed: int = 0) -> dict:
    key = jax.random.key(seed)
    ks = iter(jax.random.split(key, 48))
    f32 = jnp.float32

    def nrm(shape, scale):
        return jax.random.normal(next(ks), shape, f32) * scale

    def gain(shape):
        return 1.0 + 0.02 * jax.random.normal(next(ks), shape, f32)

    def unif(shape, lo, hi):
        return jax.random.uniform(next(ks), shape, f32, lo, hi)

    dt0 = jnp.exp(unif((DEPTH, SSM_HEADS), math.log(1e-3), math.log(1e-1)))
    a0 = unif((DEPTH, LRU_WIDTH), 0.9, 0.999) ** (1.0 / LRU_C)
    return {
        'x_prompt': nrm((BATCH, SEQ, D_MODEL), 1.0),
        'x_sample': nrm((DEC_BATCH, DEC_SEQ, D_MODEL), 1.0),
        'cache_mla_ckv': nrm((DEPTH, DEC_BATCH, PAST_LEN, MLA_KV_LORA), 1.0),
        'cache_mla_krope': nrm((DEPTH, DEC_BATCH, PAST_LEN, MLA_ROPE_DIM), 1.0),
        'state_ssm_conv': nrm((DEPTH, DEC_BATCH, SSM_CONV - 1, SSM_XBC), 1.0),
        'state_ssm': nrm((DEPTH, DEC_BATCH, SSM_HEADS, SSM_HEAD_DIM, SSM_STATE), 0.1),
        'state_lru_conv': nrm((DEPTH, DEC_BATCH, LRU_CONV - 1, LRU_WIDTH), 1.0),
        'state_lru': nrm((DEPTH, DEC_BATCH, LRU_WIDTH), 0.5),
        'state_sc_conv': nrm((DEPTH, DEC_BATCH, SC_CONV - 1, SC_WIDTH), 1.0),
        'state_ffn_conv': nrm((DEPTH, DEC_BATCH, FFN_CONV - 1, 2 * D_FF), 1.0),
        'ln_mix_g': gain((DEPTH, D_MODEL)),
        'w_in': nrm((DEPTH, D_MODEL, IN_WIDTH), D_MODEL ** -0.5),
        'q_a_norm_g': gain((DEPTH, MLA_Q_LORA)),
        'w_q_b': nrm((DEPTH, MLA_Q_LORA, MLA_HEADS * MLA_QK_DIM), MLA_Q_LORA ** -0.5),
        'kv_a_norm_g': gain((DEPTH, MLA_KV_LORA)),
        'w_kv_b': nrm((DEPTH, MLA_KV_LORA, MLA_HEADS * (MLA_NOPE_DIM + MLA_V_DIM)), MLA_KV_LORA ** -0.5),
        'q_norm_g': gain((DEPTH, MLA_QK_DIM)),
        'k_norm_g': gain((DEPTH, MLA_QK_DIM)),
        'ssm_conv_w': nrm((DEPTH, SSM_CONV, SSM_XBC), SSM_CONV ** -0.5),
        'ssm_conv_b': nrm((DEPTH, SSM_XBC), 0.02),
        'ssm_dt_bias': dt0 + jnp.log(-jnp.expm1(-dt0)),
        'ssm_a_log': jnp.log(unif((DEPTH, SSM_HEADS), 1.0, 16.0)),
        'ssm_d': gain((DEPTH, SSM_HEADS)),
        'lru_conv_w': nrm((DEPTH, LRU_CONV, LRU_WIDTH), LRU_CONV ** -0.5),
        'lru_conv_b': nrm((DEPTH, LRU_WIDTH), 0.02),
        'lru_w_a': nrm((DEPTH, LRU_BLOCKS, LRU_BLOCK_DIM, LRU_BLOCK_DIM), LRU_BLOCK_DIM ** -0.5),
        'lru_b_a': nrm((DEPTH, LRU_WIDTH), 0.02),
        'lru_w_x': nrm((DEPTH, LRU_BLOCKS, LRU_BLOCK_DIM, LRU_BLOCK_DIM), LRU_BLOCK_DIM ** -0.5),
        'lru_b_x': nrm((DEPTH, LRU_WIDTH), 0.02),
        'lru_lambda': jnp.log(a0) - jnp.log1p(-a0),
        'sc_conv_w': nrm((DEPTH, SC_CONV, SC_WIDTH), SC_CONV ** -0.5),
        'out_norm_g': gain((DEPTH, D_MIX)),
        'w_out': nrm((DEPTH, D_MIX, D_MODEL), D_MIX ** -0.5),
        'ln_ffn_g': gain((DEPTH, D_MODEL)),
        'w_ffn_up': nrm((DEPTH, D_MODEL, 2 * D_FF), D_MODEL ** -0.5),
        'ffn_conv_w': nrm((DEPTH, FFN_CONV, 2 * D_FF), FFN_CONV ** -0.5),
        'ffn_conv_b': nrm((DEPTH, 2 * D_FF), 0.02),
        'w_ffn_down': nrm((DEPTH, D_FF, D_MODEL), D_FF ** -0.5),
    }


def reference(x_prompt, x_sample, cache_mla_ckv, cache_mla_krope, state_ssm_conv, state_ssm,
              state_lru_conv, state_lru, state_sc_conv, state_ffn_conv,
              ln_mix_g, w_in, q_a_norm_g, w_q_b, kv_a_norm_g, w_kv_b, q_norm_g, k_norm_g,
              ssm_conv_w, ssm_conv_b, ssm_dt_bias, ssm_a_log, ssm_d,
              lru_conv_w, lru_conv_b, lru_w_a, lru_b_a, lru_w_x, lru_b_x, lru_lambda,
              sc_conv_w, out_norm_g, w_out, ln_ffn_g, w_ffn_up, ffn_conv_w, ffn_conv_b, w_ffn_down):
    bp = x_prompt.shape[0]
    dtp = x_prompt.dtype
    y_p, y_s = x_prompt, x_sample
    new_p, new_s = [], []
    for l in range(DEPTH):
        p = {
            'ln_mix_g': ln_mix_g[l], 'w_in': w_in[l],
            'q_a_norm_g': q_a_norm_g[l], 'w_q_b': w_q_b[l], 'kv_a_norm_g': kv_a_norm_g[l],
            'w_kv_b': w_kv_b[l], 'q_norm_g': q_norm_g[l], 'k_norm_g': k_norm_g[l],
            'ssm_conv_w': ssm_conv_w[l], 'ssm_conv_b': ssm_conv_b[l], 'ssm_dt_bias': ssm_dt_bias[l],
            'ssm_a_log': ssm_a_log[l], 'ssm_d': ssm_d[l],
            'lru_conv_w': lru_conv_w[l], 'lru_conv_b': lru_conv_b[l], 'lru_w_a': lru_w_a[l],
            'lru_b_a': lru_b_a[l], 'lru_w_x': lru_w_x[l], 'lru_b_x': lru_b_x[l], 'lru_lambda': lru_lambda[l],
            'sc_conv_w': sc_conv_w[l], 'out_norm_g': out_norm_g[l], 'w_out': w_out[l],
            'ln_ffn_g': ln_ffn_g[l], 'w_ffn_up': w_ffn_up[l], 'ffn_conv_w': ffn_conv_w[l],
            'ffn_conv_b': ffn_conv_b[l], 'w_ffn_down': w_ffn_down[l],
        }
        y_p, st_p = _layer(
            y_p, 0, None, None,
            jnp.zeros((bp, SSM_CONV - 1, SSM_XBC), dtp),
            jnp.zeros((bp, SSM_HEADS, SSM_HEAD_DIM, SSM_STATE), jnp.float32),
            jnp.zeros((bp, LRU_CONV - 1, LRU_WIDTH), dtp),
            jnp.zeros((bp, LRU_WIDTH), jnp.float32),
            jnp.zeros((bp, SC_CONV - 1, SC_WIDTH), dtp),
            jnp.zeros((bp, FFN_CONV - 1, 2 * D_FF), dtp),
            p)
        y_s, st_s = _layer(
            y_s, PAST_LEN, cache_mla_ckv[l], cache_mla_krope[l],
            state_ssm_conv[l], state_ssm[l], state_lru_conv[l], state_lru[l],
            state_sc_conv[l], state_ffn_conv[l], p)
        new_p.append(st_p)
        new_s.append(st_s)
    ckv_p, krope_p, ssm_conv_p, ssm_p, lru_conv_p, lru_p, sc_conv_p, ffn_conv_p = [
        jnp.stack([s[i] for s in new_p]) for i in range(8)]
    ckv_s, krope_s, ssm_conv_s, ssm_s, lru_conv_s, lru_s, sc_conv_s, ffn_conv_s = [
        jnp.stack([s[i] for s in new_s]) for i in range(8)]
    return (y_p, y_s,
            ckv_p, krope_p, ssm_conv_p, ssm_p, lru_conv_p, lru_p, sc_conv_p, ffn_conv_p,
            ckv_s, krope_s, ssm_conv_s, ssm_s, lru_conv_s, lru_s, sc_conv_s, ffn_conv_s)
```

```python
import contextlib
import numpy as np
import concourse.bass as bass
import concourse.mybir as mybir
from concourse.bass_utils import run_bass_kernel_spmd

F32 = mybir.dt.float32
BF16 = mybir.dt.bfloat16
AF = mybir.ActivationFunctionType
ALU = mybir.AluOpType
AX = mybir.AxisListType

D = 1024
DEPTH = 2
SEQ = 16384
BATCH = 2
DEC_BATCH = 8
DEC_SEQ = 64
PAST = 2048
DFF = 2816
EPS = 1e-6
NPIECE = 25
NFF = 44
ENG = ("pe", "act", "dve", "pool", "sp")


class Buf:
    __slots__ = ("name", "t", "w", "rd", "dsem", "dcnt")

    def __init__(self, name, t):
        self.name = name
        self.t = t
        self.w = None
        self.rd = {}
        self.dsem = None
        self.dcnt = 0

    def __getitem__(self, key):
        return self.t[key]


class View:
    def __init__(self, base, ap):
        self.base = base
        self.t = ap

    def __getitem__(self, key):
        return self.t[key]


class Op:
    __slots__ = ("fn", "waits", "signal", "dbuf")

    def __init__(self, fn, waits, dbuf):
        self.fn = fn
        self.waits = waits
        self.signal = False
        self.dbuf = dbuf


class K:
    def __init__(self, nc):
        self.nc = nc
        self.q = {e: [] for e in ENG}
        self.waited = {e: {f: -1 for f in ENG} for e in ENG}
        self.waited_d = {e: {} for e in ENG}
        self.dbufs = []
        self.same_raw = True

    def _dep_eng(self, eng, dep, waits, same_ok):
        f, idx = dep
        if f == eng:
            if eng == "pe" or not same_ok:
                return
        if self.waited[eng][f] >= idx:
            return
        self.waited[eng][f] = idx
        self.q[f][idx].signal = True
        waits.append(("e", f, idx))

    def _dep_dma(self, eng, b, waits):
        if b.dcnt == 0:
            return
        if self.waited_d[eng].get(id(b), 0) >= b.dcnt:
            return
        self.waited_d[eng][id(b)] = b.dcnt
        waits.append(("d", b, b.dcnt))

    def op(self, eng, fn, R=(), W=(), dma=None):
        R = [getattr(b, "base", b) for b in R]
        W = [getattr(b, "base", b) for b in W]
        if dma is not None:
            dma = getattr(dma, "base", dma)
        waits = []
        for b in R:
            if b.w is not None:
                self._dep_eng(eng, b.w, waits, self.same_raw)
            self._dep_dma(eng, b, waits)
        for b in W:
            if b.w is not None:
                self._dep_eng(eng, b.w, waits, False)
            for f, idx in b.rd.items():
                self._dep_eng(eng, (f, idx), waits, False)
            self._dep_dma(eng, b, waits)
        idx = len(self.q[eng])
        o = Op(fn, waits, dma)
        self.q[eng].append(o)
        if dma is not None:
            if dma.dsem is None:
                dma.dsem = len(self.dbufs)
                self.dbufs.append(dma)
            dma.dcnt += 1
        else:
            for b in R:
                b.rd[eng] = idx
            for b in W:
                b.w = (eng, idx)
                b.rd = {}
        return idx

    def barrier(self):
        last = {}
        for f in ENG:
            if f == "sp":
                continue
            for i in range(len(self.q[f]) - 1, -1, -1):
                if self.q[f][i].fn is not None:
                    last[f] = i
                    break
        for e in ENG:
            waits = []
            for f, i in last.items():
                if f == e:
                    continue
                self._dep_eng(e, (f, i), waits, False)
            for b in self.dbufs:
                self._dep_dma(e, b, waits)
            self.q[e].append(Op(None, waits, None))

    def emit(self):
        nc = self.nc
        with contextlib.ExitStack() as st:
            esem = {e: st.enter_context(nc.semaphore("sem_" + e)) for e in ENG}
            dsem = [st.enter_context(nc.semaphore("dsem%d" % i)) for i in range(len(self.dbufs))]
            sval = {}
            for e in ENG:
                c = 0
                for i, o in enumerate(self.q[e]):
                    if o.signal:
                        c += 1
                        sval[(e, i)] = c
            block = st.enter_context(nc.Block())

            def run(e, engine):
                dcount = {}
                for i, o in enumerate(self.q[e]):
                    for wt in o.waits:
                        if wt[0] == "e":
                            engine.wait_ge(esem[wt[1]], sval[(wt[1], wt[2])])
                        else:
                            engine.wait_ge(dsem[wt[1].dsem], 16 * wt[2])
                    if o.fn is None:
                        continue
                    ins = o.fn(engine)
                    if o.dbuf is not None:
                        ins.then_inc(dsem[o.dbuf.dsem], 16)
                    elif o.signal:
                        ins.then_inc(esem[e], 1)

            @block.tensor
            def _(engine):
                run("pe", engine)

            @block.scalar
            def _(engine):
                run("act", engine)

            @block.vector
            def _(engine):
                run("dve", engine)

            @block.gpsimd
            def _(engine):
                run("pool", engine)

            @block.sync
            def _(engine):
                run("sp", engine)


NCONST = 1028


def make_consts():
    c = np.zeros((128, NCONST), np.float32)
    k = np.arange(128)
    c[:, 0:128] = np.eye(128, dtype=np.float32)
    same = (k[:, None] // 64) == (k[None, :] // 64)
    c[:, 128:256] = (same & (k[:, None] > k[None, :])).astype(np.float32)
    c[:, 256:384] = (same & (k[:, None] <= k[None, :])).astype(np.float32)
    c[:, 384:512] = same.astype(np.float32)
    t = np.arange(64)
    c[:, 512:576] = ((k[:, None] % 64) <= t[None, :]).astype(np.float32)
    c[:, 576:640] = (t[None, :] >= (k[:, None] % 64)).astype(np.float32)
    sel = np.zeros((128, 2, 128), np.float32)
    sel[0:64, 0, :] = 1.0
    sel[64:128, 1, :] = 1.0
    c[:, 640:896] = sel.reshape(128, 256)
    c[:, 896:1024] = 1.0
    c[:, 1024] = EPS
    c[:, 1025] = -0.5 * np.log(96.0)
    return c


def rope_table(pos):
    half = 16
    inv = (10000.0 ** (-np.arange(half, dtype=np.float32) / half)).astype(np.float32)
    ang = pos.astype(np.float32)[:, None] * inv[None, :]
    cos, sin = np.cos(ang).astype(np.float32), np.sin(ang).astype(np.float32)
    return np.concatenate([cos, cos, -sin, sin], axis=1).astype(np.float32)


W_IN_PERM = np.concatenate([np.arange(0, 416), np.arange(1440, 1444), np.arange(416, 672),
                            np.arange(672, 1440), np.arange(1444, 2724)])
UP_PERM = np.concatenate([np.concatenate([np.arange(256 * r, 256 * r + 256),
                                          DFF + np.arange(256 * r, 256 * r + 256)]) for r in range(11)])
NROW = 652
NPP = 250
PP = dict(gm=0, gf=8, gqa=16, cws=18, cbs=42, lcw=48, lcb=56, ba=58, bx=60, lam=62, scw=64,
          goc=70, god=72, fcw=74, fcb=206)
RW = dict(gkv=0, dtb=128, alog=132, dd=136, goa=140, gob=396)


def chan(v, n):
    return np.ascontiguousarray(v.reshape(n, 128).T)


def prep_layer_params(inp, l):
    pp = np.zeros((128, NPP), np.float32)
    pp[:, 0:8] = chan(inp["ln_mix_g"][l], 8)
    pp[:, 8:16] = chan(inp["ln_ffn_g"][l], 8)
    pp[:, 16:18] = chan(inp["q_a_norm_g"][l], 2)
    cw = inp["ssm_conv_w"][l]
    pp[:, 18:42] = np.stack([chan(cw[k], 6) for k in range(4)], axis=2).reshape(128, 24)
    pp[:, 42:48] = chan(inp["ssm_conv_b"][l], 6)
    lw = inp["lru_conv_w"][l]
    pp[:, 48:56] = np.stack([chan(lw[k], 2) for k in range(4)], axis=2).reshape(128, 8)
    pp[:, 56:58] = chan(inp["lru_conv_b"][l], 2)
    pp[:, 58:60] = chan(inp["lru_b_a"][l], 2)
    pp[:, 60:62] = chan(inp["lru_b_x"][l], 2)
    pp[:, 62:64] = chan(inp["lru_lambda"][l], 2)
    sw = inp["sc_conv_w"][l]
    pp[:, 64:70] = np.stack([chan(sw[k], 2) for k in range(3)], axis=2).reshape(128, 6)
    og = inp["out_norm_g"][l]
    pp[:, 70:72] = chan(og[512:768], 2)
    pp[:, 72:74] = chan(og[768:1024], 2)
    fw = inp["ffn_conv_w"][l][:, UP_PERM]
    pp[:, 74:206] = np.stack([chan(fw[k], NFF) for k in range(3)], axis=2).reshape(128, 132)
    pp[:, 206:250] = chan(inp["ffn_conv_b"][l][UP_PERM], NFF)

    rows = np.zeros((NROW,), np.float32)
    rows[0:128] = inp["kv_a_norm_g"][l]
    gq, gk = inp["q_norm_g"][l], inp["k_norm_g"][l]
    gq_r, gk_r = gq, gk
    rows[128:132] = inp["ssm_dt_bias"][l]
    rows[132:136] = inp["ssm_a_log"][l]
    rows[136:140] = inp["ssm_d"][l]
    rows[140:396] = og[0:256]
    rows[396:652] = og[256:512]
    rows = np.ascontiguousarray(np.broadcast_to(rows[None, :], (128, NROW)))
    gqg = np.concatenate([np.tile(gq_r, 4), np.tile(gk_r, 4)])
    gqg = np.ascontiguousarray(np.broadcast_to(gqg[None, :], (128, 768)))

    bd = np.zeros((128, 4, 128), np.float32)
    for j in range(2):
        for q in range(2):
            bd[q * 64:(q + 1) * 64, j, q * 64:(q + 1) * 64] = inp["lru_w_a"][l][2 * j + q]
            bd[q * 64:(q + 1) * 64, 2 + j, q * 64:(q + 1) * 64] = inp["lru_w_x"][l][2 * j + q]
    wqb = np.ascontiguousarray(inp["w_q_b"][l].reshape(2, 128, 384).transpose(1, 0, 2))
    wkvb = inp["w_kv_b"][l].reshape(128, 4, 128)
    wkv = np.zeros((128, 512), np.float32)
    wkv[:, 0:256] = wkvb[:, :, 0:64].reshape(128, 256)
    wkv[:, 256:512] = wkvb[:, :, 64:128].reshape(128, 256)
    win = np.ascontiguousarray(inp["w_in"][l][:, W_IN_PERM].reshape(8, 128, 2724))
    wout = np.ascontiguousarray(inp["w_out"][l].reshape(8, 128, 1024))
    wup = np.ascontiguousarray(inp["w_ffn_up"][l][:, UP_PERM].reshape(8, 128, 5632))
    wdn = np.ascontiguousarray(inp["w_ffn_down"][l].reshape(22, 128, 1024))
    return dict(pp=pp, rows=rows, gqg=gqg, bd=bd, wqb=wqb, wkv=wkv, win=win, wout=wout, wup=wup, wdn=wdn)


def piece_table():
    P = [("win", 0, 8, 0, 420), ("win", 0, 8, 420, 256)]
    for i in range(4):
        P.append(("win", 0, 8, 676 + 512 * i, 512))
    for i in range(2):
        P.append(("wout", 0, 8, 512 * i, 512))
    for r in range(11):
        P.append(("wup", 0, 8, 512 * r, 512))
    for hh in range(2):
        for (k0, nk) in ((0, 8), (8, 8), (16, 6)):
            P.append(("wdn", k0, nk, 512 * hh, 512))
    return P


PIECES = piece_table()


def build(TP=SEQ, debug=False):
    nc = bass.Bass("TRN2", target_bir_lowering=False)
    k = K(nc)
    st = contextlib.ExitStack()

    def din(name, shape, dt=F32):
        return nc.dram_tensor(name, list(shape), dt, kind="ExternalInput").ap()

    def dout(name, shape, dt=F32):
        return nc.dram_tensor(name, list(shape), dt, kind="ExternalOutput").ap()

    def dscr(name, shape, dt=F32):
        return nc.dram_tensor(name, list(shape), dt, kind="Internal").ap()

    groups = [dict(name="p", T=TP, PAST=0, TW=min(512, TP)), dict(name="s", T=DEC_SEQ, PAST=PAST, TW=64)]
    for g in groups:
        g["SW"] = min(128, g["TW"])
        g["NSUB"] = g["TW"] // g["SW"]
        g["NT"] = g["T"] // g["TW"]
        g["NK"] = g["PAST"] + g["T"]
        n = g["name"]
        g["x"] = din("x_" + n, (g["T"], D))
        g["rope"] = din("rope_" + n, (g["T"], 64))
        g["ssm_conv0"] = din("ssm_conv0_" + n, (DEPTH, 128, 6, 3))
        g["ssm0"] = din("ssm0_" + n, (DEPTH, 128, 4, 64))
        g["lru_conv0"] = din("lru_conv0_" + n, (DEPTH, 128, 2, 3))
        g["lru0"] = din("lru0_" + n, (DEPTH, 128, 2))
        g["sc_conv0"] = din("sc_conv0_" + n, (DEPTH, 128, 2, 2))
        g["ffn_conv0"] = din("ffn_conv0_" + n, (DEPTH, 128, NFF, 2))
        if g["PAST"]:
            g["ckv_past"] = din("ckv_past_" + n, (DEPTH, g["PAST"], 128))
            g["kr_past"] = din("kr_past_" + n, (DEPTH, g["PAST"], 32))
        g["y"] = dout("y_" + n, (g["T"], D))
        g["ckv_o"] = dout("ckv_o_" + n, (DEPTH, g["T"], 128))
        g["kr_o"] = dout("kr_o_" + n, (DEPTH, g["T"], 32))
        g["ssm_conv_o"] = dout("ssm_conv_o_" + n, (DEPTH, 128, 6, 3))
        g["ssm_o"] = dout("ssm_o_" + n, (DEPTH, 128, 4, 64))
        g["lru_conv_o"] = dout("lru_conv_o_" + n, (DEPTH, 128, 2, 3))
        g["lru_o"] = dout("lru_o_" + n, (DEPTH, 128, 2))
        g["sc_conv_o"] = dout("sc_conv_o_" + n, (DEPTH, 128, 2, 2))
        g["ffn_conv_o"] = dout("ffn_conv_o_" + n, (DEPTH, 128, NFF, 2))
        g["x1"] = dscr("x1_" + n, (g["T"], D))
    consts_d = din("consts", (128, NCONST))
    if debug:
        dbg_mix = dout("dbg_mix", (128, 8, 512), BF16)
    L = []
    for l in range(DEPTH):
        L.append(dict(
            pp=din("pp%d" % l, (128, NPP)), rows=din("rows%d" % l, (128, NROW)), gqg=din("gqg%d" % l, (128, 768)),
            bd=din("bd%d" % l, (128, 4, 128)), wqb=din("wqb%d" % l, (128, 2, 384)),
            wkv=din("wkv%d" % l, (128, 512)), win=din("win%d" % l, (8, 128, 2724)),
            wout=din("wout%d" % l, (8, 128, 1024)), wup=din("wup%d" % l, (8, 128, 5632)),
            wdn=din("wdn%d" % l, (22, 128, 1024)),
            wscr=dscr("wscr%d" % l, (NPIECE, 128, 4096), BF16)))

    NKMAX = max(g["NK"] for g in groups)
    NBLK = (NKMAX + 127) // 128

    def sb(name, shape, dt=F32):
        return Buf(name, st.enter_context(nc.sbuf_tensor("sb_" + name, list(shape), dt)))

    def psb(name):
        return Buf(name, st.enter_context(nc.psum_tensor(name, [128, 512], F32)))

    PS = [psb("ps%d" % i) for i in range(8)]
    cst = sb("cst", (128, NCONST))
    identb = sb("identb", (128, 128), BF16)
    ppL = [sb("ppS", (128, NPP))] * DEPTH
    rowsL = [sb("rowsS", (128, NROW))] * DEPTH
    dervL = [sb("dervS", (128, 16))] * DEPTH
    gqkL = [sb("gqkS", (128, 384))] * DEPTH
    bdL = [sb("bdS", (128, 4, 128), BF16)] * DEPTH
    wqbL = [sb("wqbS", (128, 2, 384), BF16)] * DEPTH
    wkvL = [sb("wkvS", (128, 512), BF16)] * DEPTH
    ckvT = sb("ckvT", (128, NKMAX), BF16)
    krtok = sb("krtok", (128, NBLK, 32), BF16)
    skt = sb("skt", (128, NBLK, 4))
    xt = sb("xt0", (128, 4, D))
    xts = [xt, xt]
    XS = [Buf("xs%d" % i, xt.t) for i in range(4)]
    actT = sb("actT", (128, 8, 512), BF16)
    mixT = actT
    ATS = [Buf("actT_s%d" % i, actT[:, :, i * 128:(i + 1) * 128]) for i in range(4)]
    NSLOT = 3
    ring = [sb("ring%d" % i, (128, 4096), BF16) for i in range(NSLOT)]
    xbc_in = sb("xbc_in", (128, 6, 516))
    lrux_in = sb("lrux_in", (128, 2, 516))
    prod = sb("prod", (128, 2, 516))
    ffn_halo = sb("ffn_halo", (128, NFF, 2))
    hT = sb("hT", (128, 256))
    hTb = sb("hTb", (128, 256), BF16)
    lruh = sb("lruh", (128, 2))
    U = sb("U", (128, 11264), BF16)
    hff = Buf("hff", U[:, :].rearrange("p (k c) -> p k c", k=22))
    xbc_c = Buf("xbc_c", U[:, 0:3072].rearrange("p (c t) -> p c t", c=6))
    lrug = Buf("lrug", U[:, 3072:5120].bitcast(F32).rearrange("p (j t) -> p j t", j=2))
    scb = Buf("scb", U[:, 5120:7168].bitcast(F32).rearrange("p (j t) -> p j t", j=2))
    szt = Buf("szt", U[:, 7168:9216].bitcast(F32).rearrange("p (s c) -> p s c", s=4))
    QT = Buf("QT", U[0:96, 9216:11264].rearrange("p (h t) -> p h t", h=4))
    HFK = [Buf("hffk%d" % i, hff.t) for i in range(3)]

    def hff_bufs(m):
        al = xbc_c if m < 6 else lrug if m < 10 else scb if m < 14 else szt if m < 18 else QT
        return [HFK[m // 8], al]
    otok = sb("otok", (128, 4, 256))
    ATF = [sb("atf%d" % i, (128, 516)) for i in range(3)]
    ASF = [sb("asf%d" % i, (128, 16)) for i in range(4)]
    KT = [sb("KT%d" % i, (96, 512), BF16) for i in range(2)]
    VA = [sb("VA%d" % i, (128, 4, 4, 65), BF16) for i in range(2)]
    KRT = [sb("KRT%d" % i, (96, 512), BF16) for i in range(2)]
    NPT = 6
    LOOKAHEAD = 4
    PT = [sb("PT%d" % i, (128, 512), BF16) for i in range(NPT)]
    dtt = sb("dtt", (128, 4, 4))
    ckv_st = sb("ckv_st", (128, 4, 128))
    kr_st = sb("kr_st", (128, 4, 32))
    ropet0 = sb("ropet0", (128, 4, 64))
    ropets = [ropet0, ropet0]
    xstate = {"n": 0, "pre": None}
    NTMP = 11
    tmpF = [sb("tmpF%d" % i, (128, 516)) for i in range(NTMP)]
    tmpB = [sb("tmpB%d" % i, (128, 1024), BF16) for i in range(4)]
    smallF = [sb("smallF%d" % i, (128, 16)) for i in range(12)]
    cnt = {"f": 0, "b": 0, "s": 0, "ps": 0, "ring": 0, "pt": 0}

    def TF():
        cnt["f"] += 1
        return tmpF[cnt["f"] % NTMP]

    def TB():
        cnt["b"] += 1
        return tmpB[cnt["b"] % 4]

    def SF():
        cnt["s"] += 1
        return smallF[cnt["s"] % 12]

    ps_pool = list(range(8))

    def PSN():
        cnt["ps"] += 1
        return PS[ps_pool[cnt["ps"] % len(ps_pool)]]

    def dma(out, in_, buf, R=(), W=(), eng="sp"):
        k.op(eng, lambda e: e.dma_start(out=out, in_=in_), R=R, W=W, dma=buf)

    def mm(out, lhsT, rhs, start, stop, R, W):
        k.op("pe", lambda e: e.matmul(out, lhsT, rhs, start=start, stop=stop), R=R, W=W)

    def tr(out, in_, ident, R, W):
        k.op("pe", lambda e: e.transpose(out, in_, ident), R=R, W=W)

    def act(out, in_, func, R, W, bias=None, scale=None, accum=None, eng="act"):
        kw = {}
        if bias is not None:
            kw["bias"] = bias
        if scale is not None:
            kw["scale"] = scale
        if accum is not None:
            kw["accum_out"] = accum
        k.op(eng, lambda e: e.activation(out=out, in_=in_, func=func, **kw), R=R, W=W)

    def ts(out, in0, s1, s2, op0, op1, R, W, eng="dve"):
        if op1 is None:
            k.op(eng, lambda e: e.tensor_scalar(out=out, in0=in0, scalar1=s1, scalar2=None, op0=op0), R=R, W=W)
        else:
            k.op(eng, lambda e: e.tensor_scalar(out=out, in0=in0, scalar1=s1, scalar2=s2, op0=op0, op1=op1), R=R, W=W)

    def tt(out, in0, in1, op, R, W, eng="dve"):
        k.op(eng, lambda e: e.tensor_tensor(out=out, in0=in0, in1=in1, op=op), R=R, W=W)

    def stt(out, in0, scalar, in1, op0, op1, R, W, eng="dve"):
        k.op(eng, lambda e: e.scalar_tensor_tensor(out=out, in0=in0, scalar=scalar, in1=in1, op0=op0, op1=op1), R=R, W=W)

    def cp(out, in_, R, W, eng="dve"):
        k.op(eng, lambda e: e.tensor_copy(out=out, in_=in_), R=R, W=W)

    def red(out, in_, R, W):
        k.op("dve", lambda e: e.tensor_reduce(out=out, in_=in_, axis=AX.X, op=ALU.add), R=R, W=W)

    def recip(out, in_, R, W):
        k.op("dve", lambda e: e.reciprocal(out=out, in_=in_), R=R, W=W)

    def mset(ap, val, W, eng="pool"):
        k.op(eng, lambda e: e.memset(ap, val), W=W)

    ident = cst[:, 0:128]
    U2 = cst[:, 128:256]
    T2 = cst[:, 256:384]
    O2 = cst[:, 384:512]
    tri2 = cst[:, 512:576]
    mask01 = cst[:, 576:640]
    ones = cst[:, 896:1024]
    epsc = cst[:, 1024:1025]
    LN96H = cst[:, 1025:1026]

    def rstd_from(ms, n, R):
        s1 = SF()
        act(s1[0:n, 0:1], ms, AF.Ln, R=list(R) + [cst], W=[s1], bias=epsc[0:n, :])
        s2 = SF()
        act(s2[0:n, 0:1], s1[0:n, 0:1], AF.Exp, R=[s1], W=[s2], scale=-0.5)
        return s2

    def sigmoid_to(out_ap, out_buf, in_ap, Rin, n, w, scale=1.0, nbias=None, mul_ap=None, mul_R=()):
        e = TF()
        if nbias is not None:
            act(e[0:n, 0:w], in_ap, AF.Exp, R=list(Rin), W=[e], scale=-scale, bias=nbias)
        else:
            act(e[0:n, 0:w], in_ap, AF.Exp, R=list(Rin), W=[e], scale=-scale)
        ts(e[0:n, 0:w], e[0:n, 0:w], 1.0, None, ALU.add, None, R=[e], W=[e])
        if mul_ap is None:
            recip(out_ap, e[0:n, 0:w], R=[e], W=[out_buf])
        else:
            recip(e[0:n, 0:w], e[0:n, 0:w], R=[e], W=[e])
            tt(out_ap, mul_ap, e[0:n, 0:w], ALU.mult, R=list(mul_R) + [e], W=[out_buf])

    for b_ in ATF:
        mset(b_[64:66, :], 0.0, W=[b_])
    dma(cst[:, :], consts_d[:, :], cst, W=[cst])
    cp(identb[:, :], ident, R=[cst], W=[identb])
    def prep_params(l):
        Ld = L[l]
        dma(ppL[l][:, :], Ld["pp"][:, :], ppL[l], W=[ppL[l]])
        dma(rowsL[l][:, :], Ld["rows"][:, :], rowsL[l], W=[rowsL[l]])
        gt = TF()
        gt2 = TF()
        dma(gt[:, 0:384], Ld["gqg"][:, 0:384], gt, W=[gt])
        dma(gt2[:, 0:384], Ld["gqg"][:, 384:768], gt2, W=[gt2])
        tt(gqkL[l][:, :], gt[:, 0:384], gt2[:, 0:384], ALU.mult, R=[gt, gt2], W=[gqkL[l]])
        dv = dervL[l]
        t0 = SF()
        act(t0[:, 0:2], ppL[l][:, PP["lam"]:PP["lam"] + 2], AF.Exp, R=[ppL[l]], W=[t0], scale=-1.0)
        t1 = SF()
        act(t1[:, 0:2], t0[:, 0:2], AF.Ln, R=[t0], W=[t1], bias=1.0)
        ts(dv[:, 0:2], t1[:, 0:2], -8.0, None, ALU.mult, None, R=[t1], W=[dv])
        ts(dv[:, 2:4], t1[:, 0:2], -16.0, None, ALU.mult, None, R=[t1], W=[dv])
        t2 = SF()
        act(t2[:, 0:4], rowsL[l][:, RW["alog"]:RW["alog"] + 4], AF.Exp, R=[rowsL[l]], W=[t2])
        ts(dv[:, 4:8], t2[:, 0:4], -1.0, None, ALU.mult, None, R=[t2], W=[dv])
        ts(dv[:, 8:12], ppL[l][:, PP["ba"]:PP["ba"] + 4], -1.0, None, ALU.mult, None, R=[ppL[l]], W=[dv])
        f = xt
        dma(f[:, 0, 0:512].rearrange("p (a b) -> p a b", a=4), Ld["bd"][:, :, :], f, W=[f])
        cp(bdL[l][:, :, :], f[:, 0, 0:512].rearrange("p (a b) -> p a b", a=4), R=[f], W=[bdL[l]])
        dma(f[:, 1, 0:768].rearrange("p (a b) -> p a b", a=2), Ld["wqb"][:, :, :], f, W=[f])
        for kk in range(2):
            ts(wqbL[l][:, kk, :], f[:, 1, kk * 384:(kk + 1) * 384], ppL[l][:, PP["gqa"] + kk:PP["gqa"] + kk + 1], None,
               ALU.mult, None, R=[f, ppL[l]], W=[wqbL[l]])
        dma(f[:, 2, 0:512], Ld["wkv"][:, :], f, W=[f])
        cp(wkvL[l][:, :], f[:, 2, 0:512], R=[f], W=[wkvL[l]])

    for l in range(DEPTH):
        Ld = L[l]
        prep_params(l)
        for pi, (src, k0, nk, c0, ncol) in enumerate(PIECES):
            if pi % 2 == 0:
                stg = xt
                stgf = stg[:, :, :].rearrange("p a b -> p (a b)")
            else:
                stg = hff
                stgf = U[:, 0:8192].bitcast(F32)
            stgv = stgf[:, 0:nk * ncol].rearrange("p (k c) -> p k c", k=nk)
            dma(stgv, Ld[src][k0:k0 + nk, :, c0:c0 + ncol].rearrange("k p c -> p k c"), stg, W=[stg])
            slot = ring[pi % NSLOT]
            slv = slot[:, 0:nk * ncol].rearrange("p (k c) -> p k c", k=nk)
            for kk in range(nk):
                on_act = (kk % 2 == 1)
                if src in ("win", "wup"):
                    gc = PP["gm"] if src == "win" else PP["gf"]
                    if on_act:
                        act(slv[:, kk, :], stgv[:, kk, :], AF.Identity, R=[stg, ppL[l]], W=[slot], scale=ppL[l][:, gc + kk:gc + kk + 1])
                    else:
                        ts(slv[:, kk, :], stgv[:, kk, :], ppL[l][:, gc + kk:gc + kk + 1], None, ALU.mult, None,
                           R=[stg, ppL[l]], W=[slot])
                else:
                    if on_act:
                        act(slv[:, kk, :], stgv[:, kk, :], AF.Copy, R=[stg], W=[slot])
                    else:
                        cp(slv[:, kk, :], stgv[:, kk, :], R=[stg], W=[slot])
            dma(Ld["wscr"][pi, :, 0:nk * ncol], slot[:, 0:nk * ncol], slot, R=[slot])
    k.barrier()

    def load_piece(l, pi, slot=None):
        src, k0, nk, c0, ncol = PIECES[pi]
        if slot is None:
            cnt["ring"] += 1
            slot = ring[cnt["ring"] % NSLOT]
        dma(slot[:, 0:nk * ncol], L[l]["wscr"][pi, :, 0:nk * ncol], slot, W=[slot])
        return slot, slot[:, 0:nk * ncol].rearrange("p (k c) -> p k c", k=nk)

    def make_keys(l, n, ckv_ap, kr_ap, Rb, kpos, banks=None):
        blk = kpos // 128
        r0 = kpos % 128
        cb = TB()
        cp(cb[0:n, 0:128], ckv_ap, R=Rb, W=[cb])
        p = banks[0] if banks else PSN()
        pv = p[:, :].bitcast(BF16)
        tr(pv[:, 0:n], cb[0:n, 0:128], identb[0:n, 0:n], R=[cb, identb], W=[p])
        cp(ckvT[:, kpos:kpos + n], pv[:, 0:n], R=[p], W=[ckvT])
        cp(krtok[r0:r0 + n, blk, :], kr_ap, R=Rb, W=[krtok], eng="pool")
        p2 = banks[1] if banks else PSN()
        mm(p2[0:n, 0:256], ckvT[:, kpos:kpos + n], wkvL[l][:, 0:256], True, True, R=[ckvT, wkvL[l]], W=[p2])
        sq = TF()
        act(sq[0:n, 0:256], p2[0:n, 0:256], AF.Square, R=[p2], W=[sq])
        s4 = SF()
        red(s4[0:n, 0:4], sq[0:n, 0:256].rearrange("p (h c) -> p h c", h=4), R=[sq], W=[s4])
        jk = TF()
        s1 = SF()
        act(jk[0:n, 0:32], kr_ap, AF.Square, R=Rb, W=[jk, s1], accum=s1[0:n, 0:1])
        s5 = SF()
        ts(s5[0:n, 0:4], s4[0:n, 0:4], s1[0:n, 0:1], 1.0 / 96.0, ALU.add, ALU.mult, R=[s4, s1], W=[s5])
        s6 = SF()
        act(s6[0:n, 0:4], s5[0:n, 0:4], AF.Ln, R=[s5, cst], W=[s6], bias=epsc[0:n, :])
        act(skt[r0:r0 + n, blk, :], s6[0:n, 0:4], AF.Exp, R=[s6], W=[skt], scale=-0.5, bias=LN96H[0:n, :])

    def attention_gen(l, g, i):
        TW, SW, NSUB = g["TW"], g["SW"], g["NSUB"]
        kq0 = g["PAST"] + i * TW
        nfull = kq0 // 512
        assert kq0 % 512 == 0
        O = [PS[0], PS[1]]
        ac = {"ps": 0, "tf": 0, "sf": 0}

        def APS():
            ac["ps"] += 1
            return PS[2 + ac["ps"] % 4]

        def ATFn():
            ac["tf"] += 1
            return ATF[ac["tf"] % 3]

        def ASFn():
            ac["sf"] += 1
            return ASF[ac["sf"] % 4]

        nsb_d = (TW + 127) // 128
        n_pv = nfull * 4 + sum(1 + (1 if j * 128 + 64 < TW else 0) for j in range(nsb_d))

        def build_kv(k0, kw, hp):
            nsb = (kw + 127) // 128
            sw = min(128, kw)
            va = VA[(k0 // 512) % 2]
            p = APS()
            for j in range(nsb):
                mm(p[0:sw, j * 128:(j + 1) * 128], ckvT[:, k0 + j * 128:k0 + j * 128 + sw],
                   wkvL[l][:, 256 + hp * 128:256 + (hp + 1) * 128], True, True, R=[ckvT, wkvL[l]], W=[p])
            cp(va[0:sw, 0:nsb, 0:2, 0:64], p[0:sw, 0:nsb * 128].rearrange("p (j h c) -> p j h c", j=nsb, h=2), R=[p], W=[va])
            return va

        def build_krT(k0, kw):
            nsb = (kw + 127) // 128
            p = APS()
            pv = p[:, :].bitcast(BF16)
            for j in range(nsb):
                sw = min(128, kw - j * 128)
                blk = (k0 + j * 128) // 128
                tr(pv[64:96, j * 128:j * 128 + sw], krtok[0:sw, blk, :], identb[0:sw, 0:sw], R=[krtok, identb], W=[p])
            krT = KRT[(k0 // 512) % 2]
            cp(krT[64:96, 0:kw], pv[64:96, 0:kw], R=[p], W=[krT])
            return krT

        def build_kt(k0, kw, h, krT):
            kt = KT[h % 2]
            p = APS()
            mm(p[0:64, 0:kw], wkvL[l][:, h * 64:(h + 1) * 64], ckvT[:, k0:k0 + kw], True, True, R=[ckvT, wkvL[l]], W=[p])
            cp(kt[0:64, 0:kw], p[0:64, 0:kw], R=[p], W=[kt])
            cp(kt[64:96, 0:kw], krT[64:96, 0:kw], R=[krT], W=[kt])
            return kt

        for hp in range(2):
            started = [False, False]
            done = [0, 0]
            pend = []

            def s_exp_pv(kt, va, hl, j, krow, qlo, qhi, blk):
                h = 2 * hp + hl
                nq = qhi - qlo
                p = APS()
                mm(p[0:krow, 0:nq], kt[:, j * 128:j * 128 + krow], QT[:, h, qlo:qhi], True, True, R=[kt, QT], W=[p])
                cnt["pt"] += 1
                pt = PT[cnt["pt"] % NPT]
                act(pt[0:krow, 0:nq], p[0:krow, 0:nq], AF.Exp, R=[p, skt], W=[pt], scale=skt[0:krow, blk, h:h + 1])

                def pv():
                    done[hl] += 1
                    mm(O[hl][0:65, qlo:qhi], va[0:krow, j, hl, :], pt[0:krow, 0:nq], not started[hl], done[hl] == n_pv,
                       R=[va, pt], W=[O[hl]])
                    started[hl] = True
                pend.append(pv)
                while len(pend) > LOOKAHEAD:
                    pend.pop(0)()

            blocks = [(kb * 512, 512, True) for kb in range(nfull)] + [(kq0, TW, False)]
            segs = []
            for bi, (k0, kw, full) in enumerate(blocks):
                for hl in range(2):
                    us = []
                    if full:
                        for j in range(4):
                            us.append((j, 128, 0, TW, (k0 // 128) + j))
                    else:
                        for j in range(nsb_d):
                            blk = (kq0 + j * 128) // 128
                            if j * 128 + 64 < TW:
                                us.append((j, 128, j * 128 + 64, TW, blk))
                            us.append((j, 64, j * 128, j * 128 + 64, blk))
                    segs.append((bi, k0, kw, hl, us))
            vas, krTs, kts = {}, {}, {}
            vas[0] = build_kv(blocks[0][0], blocks[0][1], hp)
            krTs[0] = build_krT(blocks[0][0], blocks[0][1])
            kts[0] = build_kt(blocks[0][0], blocks[0][1], 2 * hp + 0, krTs[0])
            yield
            for si, (bi, k0, kw, hl, us) in enumerate(segs):
                nxt = segs[si + 1] if si + 1 < len(segs) else None
                if nxt is not None and nxt[0] != bi:
                    vas[nxt[0]] = build_kv(nxt[1], nxt[2], hp)
                    krTs[nxt[0]] = build_krT(nxt[1], nxt[2])
                for ui, (j, krow, qlo, qhi, blk) in enumerate(us):
                    s_exp_pv(kts[si], vas[bi], hl, j, krow, qlo, qhi, blk)
                    if ui == 0 and nxt is not None:
                        kts[si + 1] = build_kt(nxt[1], nxt[2], 2 * hp + nxt[3], krTs[nxt[0]])
                    yield
            while pend:
                pend.pop(0)()
            ots = [APS() for s in range(NSUB)]
            for hl in range(2):
                ob = ATFn()
                cp(ob[0:65, 0:TW], O[hl][0:65, 0:TW], R=[O[hl]], W=[ob])
                for s in range(NSUB):
                    tr(ots[s][0:SW, hl * 66:(hl + 1) * 66], ob[0:66, s * SW:(s + 1) * SW], ident[0:66, 0:66], R=[ob, cst], W=[ots[s]])
                yield
            for s in range(NSUB):
                pv_ = ots[s][0:SW, 0:132].rearrange("p (h c) -> p h c", h=2)
                rs = ASFn()
                recip(rs[0:SW, 0:2], pv_[:, :, 64], R=[ots[s]], W=[rs])
                tt(otok[0:SW, s, hp * 128:(hp + 1) * 128].rearrange("p (h c) -> p h c", h=2), pv_[:, :, 0:64],
                   rs[0:SW, 0:2].unsqueeze(2).to_broadcast([SW, 2, 64]), ALU.mult, R=[ots[s], rs], W=[otok])
            yield

    def group_norm_tok(l, o, SW, s, kslot, goff, pbank=None):
        jk = TF()
        ms = SF()
        act(jk[0:SW, 0:256], o[0:SW, 0:256], AF.Square, R=[o], W=[jk, ms], scale=1.0 / 16.0, accum=ms[0:SW, 0:1])
        rs = rstd_from(ms[0:SW, 0:1], SW, [ms])
        ob = TB()
        stt(ob[0:SW, 0:256], o[0:SW, 0:256], rs[0:SW, 0:1], rowsL[l][0:SW, goff:goff + 256], ALU.mult, ALU.mult,
            R=[o, rs, rowsL[l]], W=[ob])
        p = pbank if pbank is not None else PSN()
        pv = p[:, :].bitcast(BF16)
        for j in range(2):
            tr(pv[:, j * SW:(j + 1) * SW], ob[0:SW, j * 128:(j + 1) * 128], identb[0:SW, 0:SW], R=[ob, identb], W=[p])
        cp(mixT[:, kslot:kslot + 2, s * SW:(s + 1) * SW], pv[:, 0:2 * SW].rearrange("p (j c) -> p j c", j=2),
           R=[p], W=[ATS[s]])

    def group_norm_fm(l, ybuf, TW, kslot, gcol):
        p = PSN()
        for j in range(2):
            sq = TF()
            act(sq[:, 0:TW], ybuf[:, j, 0:TW], AF.Square, R=[ybuf], W=[sq], eng="act")
            mm(p[:, 0:TW], ones, sq[:, 0:TW], j == 0, j == 1, R=[cst, sq], W=[p])
        r1 = TF()
        act(r1[:, 0:TW], p[:, 0:TW], AF.Ln, R=[p, cst], W=[r1], scale=1.0 / 256.0, bias=epsc)
        r2 = TF()
        act(r2[:, 0:TW], r1[:, 0:TW], AF.Exp, R=[r1], W=[r2], scale=-0.5)
        for j in range(2):
            stt(mixT[:, kslot + j, 0:TW], ybuf[:, j, 0:TW], ppL[l][:, gcol + j:gcol + j + 1], r2[:, 0:TW], ALU.mult, ALU.mult,
                R=[ybuf, ppL[l], r2], W=ATS)

    def conv_fm(inb, c, TW, H, wcol, bcol, l, outap, outbuf, eng="dve"):
        pl = ppL[l]
        if bcol is not None:
            ts(outap, inb[:, c, H:H + TW], pl[:, wcol + H:wcol + H + 1], pl[:, bcol:bcol + 1], ALU.mult, ALU.add,
               R=[inb, pl], W=[outbuf], eng=eng)
        else:
            ts(outap, inb[:, c, H:H + TW], pl[:, wcol + H:wcol + H + 1], None, ALU.mult, None, R=[inb, pl], W=[outbuf], eng=eng)
        for kk in range(H - 1, -1, -1):
            if eng == "dve":
                stt(outap, inb[:, c, kk:kk + TW], pl[:, wcol + kk:wcol + kk + 1], outap, ALU.mult, ALU.add,
                    R=[inb, pl, outbuf], W=[outbuf], eng=eng)
            else:
                tmp = TF()
                ts(tmp[:, 0:TW], inb[:, c, kk:kk + TW], pl[:, wcol + kk:wcol + kk + 1], None, ALU.mult, None,
                   R=[inb, pl], W=[tmp], eng=eng)
                tt(outap, outap, tmp[:, 0:TW], ALU.add, R=[outbuf, tmp], W=[outbuf], eng=eng)

    def tile(l, g, i):
        TW, SW, NSUB = g["TW"], g["SW"], g["NSUB"]
        n = g["name"]
        pl, rw, dv = ppL[l], rowsL[l], dervL[l]
        xsrc = g["x"] if l == 0 else g["x1"]
        xdst = g["x1"] if l == 0 else g["y"]
        r0 = i * TW

        def load_x(ii, slot):
            rr = ii * TW
            for s_ in range(NSUB):
                dma(xts[slot][0:SW, s_, :], xsrc[rr + s_ * SW:rr + (s_ + 1) * SW, :], XS[s_], W=[XS[s_]])
            dma(ropets[slot][0:SW, 0:NSUB, :], g["rope"][rr:rr + TW, :].rearrange("(s p) d -> p s d", p=SW), ropets[slot],
                W=[ropets[slot]])

        if xstate["pre"] == (l, n, i):
            slot = xstate["slot"]
        else:
            xstate["n"] += 1
            slot = xstate["n"] % 2
            load_x(i, slot)
        xt = xts[slot]
        ropet = ropets[slot]
        if xts[0] is not xts[1] and i + 1 < g["NT"]:
            xstate["pre"] = (l, n, i + 1)
            xstate["slot"] = 1 - slot
            load_x(i + 1, 1 - slot)

        def norm_T(gcol_unused):
            for s in range(NSUB):
                jk = TB()
                ms = SF()
                act(jk[0:SW, :], xt[0:SW, s, :], AF.Square, R=[XS[s]], W=[jk, ms], scale=1.0 / 32.0, accum=ms[0:SW, 0:1])
                rs = rstd_from(ms[0:SW, 0:1], SW, [ms])
                xn = TB()
                ts(xn[0:SW, :], xt[0:SW, s, :], rs[0:SW, 0:1], None, ALU.mult, None, R=[XS[s], rs], W=[xn])
                p = PSN()
                pv = p[:, :].bitcast(BF16)
                for kk in range(8):
                    tr(pv[:, kk * SW:(kk + 1) * SW], xn[0:SW, kk * 128:(kk + 1) * 128], identb[0:SW, 0:SW],
                       R=[xn, identb], W=[p])
                cp(actT[:, :, s * SW:(s + 1) * SW], pv[:, 0:8 * SW].rearrange("p (k c) -> p k c", k=8), R=[p], W=[ATS[s]])

        norm_T(None)
        slotA, wA = load_piece(l, 0, ring[0])
        slotZ, wZ = load_piece(l, 1, ring[1])
        def tokmajor_gen():
            for s in range(NSUB):
                pa = PS[0]
                for kk in range(8):
                    mm(pa[0:SW, 0:420], actT[:, kk, s * SW:(s + 1) * SW], wA[:, kk, :], kk == 0, kk == 7, R=[ATS[s], slotA], W=[pa])
                pz = PS[1]
                for kk in range(8):
                    mm(pz[0:SW, 0:256], actT[:, kk, s * SW:(s + 1) * SW], wZ[:, kk, :], kk == 0, kk == 7, R=[ATS[s], slotZ], W=[pz])
                sigmoid_to(szt[0:SW, s, :], szt, pz[0:SW, 0:256], [pz], SW, 256, mul_ap=pz[0:SW, 0:256], mul_R=[pz])
                d1 = SF()
                tt(d1[0:SW, 0:4], pa[0:SW, 416:420], rw[0:SW, RW["dtb"]:RW["dtb"] + 4], ALU.add, R=[pa, rw], W=[d1])
                d2 = SF()
                act(d2[0:SW, 0:4], d1[0:SW, 0:4], AF.Exp, R=[d1], W=[d2])
                act(dtt[0:SW, s, :], d2[0:SW, 0:4], AF.Ln, R=[d2], W=[dtt], bias=1.0)
                yield
                jk = TF()
                msq = SF()
                act(jk[0:SW, 0:256], pa[0:SW, 0:256], AF.Square, R=[pa], W=[jk, msq], scale=1.0 / 16.0, accum=msq[0:SW, 0:1])
                jk2 = TF()
                msk = SF()
                act(jk2[0:SW, 0:128], pa[0:SW, 256:384], AF.Square, R=[pa], W=[jk2, msk], scale=128.0 ** -0.5,
                    accum=msk[0:SW, 0:1])
                rq = rstd_from(msq[0:SW, 0:1], SW, [msq])
                rk = rstd_from(msk[0:SW, 0:1], SW, [msk])
                stt(ckv_st[0:SW, s, :], pa[0:SW, 256:384], rk[0:SW, 0:1], rw[0:SW, RW["gkv"]:RW["gkv"] + 128], ALU.mult, ALU.mult,
                    R=[pa, rk, rw], W=[ckv_st])
                tt(kr_st[0:SW, s, :], pa[0:SW, 384:416], ropet[0:SW, s, 0:32], ALU.mult, R=[pa, ropet], W=[kr_st])
                rbt = TF()
                tt(rbt[0:SW, 0:16], pa[0:SW, 400:416], ropet[0:SW, s, 32:48], ALU.mult, R=[pa, ropet], W=[rbt])
                tt(rbt[0:SW, 16:32], pa[0:SW, 384:400], ropet[0:SW, s, 48:64], ALU.mult, R=[pa, ropet], W=[rbt])
                tt(kr_st[0:SW, s, :], kr_st[0:SW, s, :], rbt[0:SW, 0:32], ALU.add, R=[kr_st, rbt], W=[kr_st])
                qn = TB()
                ts(qn[0:SW, 0:256], pa[0:SW, 0:256], rq[0:SW, 0:1], None, ALU.mult, None, R=[pa, rq], W=[qn])
                yield
                pq = PS[2]
                pqv = pq[:, :].bitcast(BF16)
                for j in range(2):
                    tr(pqv[:, j * SW:(j + 1) * SW], qn[0:SW, j * 128:(j + 1) * 128], identb[0:SW, 0:SW], R=[qn, identb], W=[pq])
                qnT = TB()
                cp(qnT[:, 0:2 * SW], pqv[:, 0:2 * SW], R=[pq], W=[qnT])
                yield
                pq2 = PS[3]
                for j in range(2):
                    mm(pq2[0:SW, 0:384], qnT[:, j * SW:(j + 1) * SW], wqbL[l][:, j, :], j == 0, j == 1, R=[qnT, wqbL[l]], W=[pq2])
                qv = pq2[0:SW, 0:384].rearrange("p (h c) -> p h c", h=4)
                qf = TF()
                qfv = qf[0:SW, 0:384].rearrange("p (h c) -> p h c", h=4)
                cosb = ropet[0:SW, s, 0:32].unsqueeze(1).to_broadcast([SW, 4, 32])
                tt(qfv[:, :, 64:96], qv[:, :, 64:96], cosb, ALU.mult, R=[pq2, ropet], W=[qf])
                q2 = TF()
                q2v = q2[0:SW, 0:128].rearrange("p (h c) -> p h c", h=4)
                tt(q2v[:, :, 0:16], qv[:, :, 80:96], ropet[0:SW, s, 32:48].unsqueeze(1).to_broadcast([SW, 4, 16]), ALU.mult,
                   R=[pq2, ropet], W=[q2])
                tt(q2v[:, :, 16:32], qv[:, :, 64:80], ropet[0:SW, s, 48:64].unsqueeze(1).to_broadcast([SW, 4, 16]), ALU.mult,
                   R=[pq2, ropet], W=[q2])
                tt(qfv[:, :, 64:96], qfv[:, :, 64:96], q2v, ALU.add, R=[qf, q2], W=[qf])
                cp(qfv[:, :, 0:64], qv[:, :, 0:64], R=[pq2], W=[qf])
                sq = TF()
                tt(sq[0:SW, 0:384], qf[0:SW, 0:384], qf[0:SW, 0:384], ALU.mult, R=[qf], W=[sq])
                s4 = SF()
                red(s4[0:SW, 0:4], sq[0:SW, 0:384].rearrange("p (h c) -> p h c", h=4), R=[sq], W=[s4])
                s5 = SF()
                act(s5[0:SW, 0:4], s4[0:SW, 0:4], AF.Ln, R=[s4, cst], W=[s5], scale=1.0 / 96.0, bias=epsc[0:SW, :])
                s6 = SF()
                act(s6[0:SW, 0:4], s5[0:SW, 0:4], AF.Exp, R=[s5], W=[s6], scale=-0.5)
                q3 = TF()
                tt(q3[0:SW, 0:384].rearrange("p (h c) -> p h c", h=4), qfv, s6[0:SW, 0:4].unsqueeze(2).to_broadcast([SW, 4, 96]),
                   ALU.mult, R=[qf, s6], W=[q3])
                qb = TB()
                tt(qb[0:SW, 0:384], q3[0:SW, 0:384], gqkL[l][0:SW, :], ALU.mult, R=[q3, gqkL[l]], W=[qb])
                yield
                pt_ = PS[1]
                ptv = pt_[:, :].bitcast(BF16)
                for h in range(4):
                    tr(ptv[0:96, h * SW:(h + 1) * SW], qb[0:SW, h * 96:(h + 1) * 96], identb[0:SW, 0:SW], R=[qb, identb], W=[pt_])
                cp(QT[:, :, s * SW:(s + 1) * SW], ptv[0:96, 0:4 * SW].rearrange("p (h c) -> p h c", h=4), R=[pt_], W=[QT])
                yield
                make_keys(l, SW, ckv_st[0:SW, s, :], kr_st[0:SW, s, :], [ckv_st, kr_st], g["PAST"] + r0 + s * SW, banks=(PS[2], PS[3]))
                yield

        def fm_gen():
            for pi in range(4):
                slot, wv = load_piece(l, 2 + pi, ring[2])
                for cc in range(4):
                    m = pi * 4 + cc
                    p = PS[4 + m % 4]
                    for kk in range(8):
                        mm(p[:, 0:TW], wv[:, kk, cc * 128:(cc + 1) * 128], actT[:, kk, 0:TW], kk == 0, kk == 7, R=[slot] + ATS, W=[p])
                    if m < 6:
                        act(xbc_in[:, m, 3:3 + TW], p[:, 0:TW], AF.Copy, R=[p], W=[xbc_in])
                    elif m < 8:
                        act(lrux_in[:, m - 6, 3:3 + TW], p[:, 0:TW], AF.Copy, R=[p], W=[lrux_in])
                    elif m < 10:
                        act(lrug[:, m - 8, 0:TW], p[:, 0:TW], AF.Copy, R=[p], W=[lrug])
                    elif m < 12:
                        act(scb[:, m - 10, 0:TW], p[:, 0:TW], AF.Copy, R=[p], W=[scb])
                    elif m < 14:
                        act(prod[:, m - 12, 2:2 + TW], p[:, 0:TW], AF.Copy, R=[p], W=[prod])
                    else:
                        tt(prod[:, m - 14, 2:2 + TW], prod[:, m - 14, 2:2 + TW], p[:, 0:TW], ALU.mult, R=[prod, p], W=[prod])
                    yield


        gt, gf = tokmajor_gen(), fm_gen()
        t_done = f_done = False
        acc2 = 0.0
        while not (t_done and f_done):
            if not t_done:
                try:
                    next(gt)
                except StopIteration:
                    t_done = True
            acc2 += 0.6 if not t_done else 1.0
            while acc2 >= 1.0 and not f_done:
                acc2 -= 1.0
                try:
                    next(gf)
                except StopIteration:
                    f_done = True
            if f_done:
                acc2 = 0.0
        dma(g["ckv_o"][l, r0:r0 + TW, :].rearrange("(s p) d -> p s d", p=SW), ckv_st[0:SW, 0:NSUB, :], ckv_st, R=[ckv_st])
        dma(g["kr_o"][l, r0:r0 + TW, :].rearrange("(s p) d -> p s d", p=SW), kr_st[0:SW, 0:NSUB, :], kr_st, R=[kr_st])


        def mixers_gen():
            PA, PB = PS[6], PS[7]

            for c in range(6):
                cv = TF()
                conv_fm(xbc_in, c, TW, 3, PP["cws"] + 4 * c, PP["cbs"] + c, l, cv[:, 0:TW], cv, eng="dve")
                sigmoid_to(xbc_c[:, c, 0:TW], xbc_c, cv[:, 0:TW], [cv], 128, TW, mul_ap=cv[:, 0:TW], mul_R=[cv])
                cp(xbc_in[:, c, 0:3], xbc_in[:, c, TW:TW + 3], R=[xbc_in], W=[xbc_in], eng="pool")
                yield
            nch = SW // 64
            for s in range(NSUB):
                c0 = s * SW
                p = PA
                pv = p[:, :].bitcast(BF16)
                for c in range(4):
                    tr(pv[0:SW, c * 128:(c + 1) * 128], xbc_c[:, c, c0:c0 + SW], identb[:, :], R=[xbc_c, identb], W=[p])
                tok = TB()
                cp(tok[0:SW, 0:512], pv[0:SW, 0:512], R=[p], W=[tok])
                dt = dtt[0:SW, s, :]
                dta = SF()
                tt(dta[0:SW, 0:4], dt, dv[0:SW, 4:8], ALU.mult, R=[dtt, dv], W=[dta])
                pc = PB
                mm(pc[0:SW, 0:4], T2[0:SW, 0:SW], dta[0:SW, 0:4], True, True, R=[cst, dta], W=[pc])
                mm(pc[0:SW, 4:8], O2[0:SW, 0:SW], dta[0:SW, 0:4], True, True, R=[cst, dta], W=[pc])
                for c in range(nch):
                    mm(pc[:, 8 + 4 * c:12 + 4 * c], cst[0:SW, 640 + 128 * c:768 + 128 * c], dta[0:SW, 0:4], True, True,
                       R=[cst, dta], W=[pc])
                cum = SF()
                cp(cum[0:SW, 0:8], pc[0:SW, 0:8], R=[pc], W=[cum])
                yield
                ecum = SF()
                act(ecum[0:SW, 0:4], cum[0:SW, 0:4], AF.Exp, R=[cum], W=[ecum])
                wd = SF()
                tt(wd[0:SW, 0:4], cum[0:SW, 4:8], cum[0:SW, 0:4], ALU.subtract, R=[cum], W=[wd])
                we = SF()
                act(we[0:SW, 0:4], wd[0:SW, 0:4], AF.Exp, R=[wd], W=[we])
                wend = SF()
                tt(wend[0:SW, 0:4], we[0:SW, 0:4], dt, ALU.mult, R=[we, dtt], W=[wend])
                dec = SF()
                act(dec[:, 0:4 * nch], pc[:, 8:8 + 4 * nch], AF.Exp, R=[pc], W=[dec])
                yield
                pseg = PA
                for h in range(4):
                    a2 = TF()
                    ts(a2[0:SW, 0:SW], U2[0:SW, 0:SW], dta[0:SW, h:h + 1], None, ALU.mult, None, R=[cst, dta], W=[a2])
                    mm(pseg[0:SW, h * 64:(h + 1) * 64], a2[0:SW, 0:SW], tri2[0:SW, :], True, True, R=[a2, cst], W=[pseg])
                lex = TF()
                act(lex[0:SW, 0:256], pseg[0:SW, 0:256], AF.Exp, R=[pseg], W=[lex])
                dm = TF()
                tt(dm[0:SW, 0:256].rearrange("p (h t) -> p h t", h=4), mask01[0:SW, :].unsqueeze(1).to_broadcast([SW, 4, 64]),
                   dt.unsqueeze(2).to_broadcast([SW, 4, 64]), ALU.mult, R=[cst, dtt], W=[dm])
                tt(lex[0:SW, 0:256], lex[0:SW, 0:256], dm[0:SW, 0:256], ALU.mult, R=[lex, dm], W=[lex])
                yield
                psc = PB
                for c in range(nch):
                    for gg in range(2):
                        mm(psc[c * 64:(c + 1) * 64, gg * 64:(gg + 1) * 64], xbc_c[:, 2 + gg, c0 + c * 64:c0 + (c + 1) * 64],
                           xbc_c[:, 4 + gg, c0 + c * 64:c0 + (c + 1) * 64], True, True, R=[xbc_c], W=[psc])
                mb = TB()
                tt(mb[0:SW, 0:256].rearrange("p (g r t) -> p g r t", g=2, r=2),
                   lex[0:SW, 0:256].rearrange("p (g r t) -> p g r t", g=2, r=2),
                   psc[0:SW, 0:128].rearrange("p (g t) -> p g t", g=2).unsqueeze(2).to_broadcast([SW, 2, 2, 64]),
                   ALU.mult, R=[lex, psc], W=[mb])
                xw = TB()
                tt(xw[0:SW, 0:256].rearrange("p (h c) -> p h c", h=4), tok[0:SW, 0:256].rearrange("p (h c) -> p h c", h=4),
                   wend[0:SW, 0:4].unsqueeze(2).to_broadcast([SW, 4, 64]), ALU.mult, R=[tok, wend], W=[xw])
                yield
                py = PA
                for c in range(nch):
                    rs_ = slice(c * 64, (c + 1) * 64)
                    for h in range(4):
                        mm(py[rs_, h * 64:(h + 1) * 64], mb[rs_, h * 64:(h + 1) * 64], tok[rs_, h * 64:(h + 1) * 64], True, True,
                           R=[mb, tok], W=[py])
                    for gg in range(2):
                        mm(py[rs_, 256 + gg * 128:256 + (gg + 1) * 128], xbc_c[:, 4 + gg, c0 + c * 64:c0 + (c + 1) * 64],
                           hTb[:, gg * 128:(gg + 1) * 128], True, True, R=[xbc_c, hTb], W=[py])
                    pst = PB
                    for gg in range(2):
                        mm(pst[:, gg * 128:(gg + 1) * 128], tok[rs_, 256 + gg * 128:256 + (gg + 1) * 128],
                           xw[rs_, gg * 128:(gg + 1) * 128], True, True, R=[tok, xw], W=[pst])
                    tt(hT[:, :].rearrange("p (h c) -> p h c", h=4), hT[:, :].rearrange("p (h c) -> p h c", h=4),
                       dec[:, 4 * c:4 * c + 4].unsqueeze(2).to_broadcast([128, 4, 64]), ALU.mult, R=[hT, dec], W=[hT])
                    tt(hT[:, :], hT[:, :], pst[:, 0:256], ALU.add, R=[hT, pst], W=[hT])
                    cp(hTb[:, :], hT[:, :], R=[hT], W=[hTb], eng="pool")
                y1 = TF()
                tt(y1[0:SW, 0:256].rearrange("p (h c) -> p h c", h=4), py[0:SW, 256:512].rearrange("p (h c) -> p h c", h=4),
                   ecum[0:SW, 0:4].unsqueeze(2).to_broadcast([SW, 4, 64]), ALU.mult, R=[py, ecum], W=[y1])
                tt(y1[0:SW, 0:256], y1[0:SW, 0:256], py[0:SW, 0:256], ALU.add, R=[y1, py], W=[y1])
                y2 = TF()
                tt(y2[0:SW, 0:256].rearrange("p (h c) -> p h c", h=4), tok[0:SW, 0:256].rearrange("p (h c) -> p h c", h=4),
                   rw[0:SW, RW["dd"]:RW["dd"] + 4].unsqueeze(2).to_broadcast([SW, 4, 64]), ALU.mult, R=[tok, rw], W=[y2])
                tt(y2[0:SW, 0:256], y2[0:SW, 0:256], y1[0:SW, 0:256], ALU.add, R=[y2, y1], W=[y2])
                tt(y2[0:SW, 0:256], y2[0:SW, 0:256], szt[0:SW, s, :], ALU.mult, R=[y2, szt], W=[y2])
                group_norm_tok(l, y2, SW, s, 2, RW["gob"], pbank=PB)
                yield

            for j in range(2):
                xc = TF()
                conv_fm(lrux_in, j, TW, 3, PP["lcw"] + 4 * j, PP["lcb"] + j, l, xc[:, 0:TW], xc)
                cp(lrux_in[:, j, 0:3], lrux_in[:, j, TW:TW + 3], R=[lrux_in], W=[lrux_in], eng="pool")
                xcb = TB()
                cp(xcb[:, 0:TW], xc[:, 0:TW], R=[xc], W=[xcb], eng="pool")
                yield
                pr = PA
                mm(pr[:, 0:TW], bdL[l][:, j, :], xcb[:, 0:TW], True, True, R=[bdL[l], xcb], W=[pr])
                pi_ = PB
                mm(pi_[:, 0:TW], bdL[l][:, 2 + j, :], xcb[:, 0:TW], True, True, R=[bdL[l], xcb], W=[pi_])
                r = TF()
                sigmoid_to(r[:, 0:TW], r, pr[:, 0:TW], [pr, dv], 128, TW, nbias=dv[:, 8 + j:9 + j])
                ig = TF()
                sigmoid_to(ig[:, 0:TW], ig, pi_[:, 0:TW], [pi_, dv], 128, TW, nbias=dv[:, 10 + j:11 + j])
                a2 = TF()
                act(a2[:, 0:TW], r[:, 0:TW], AF.Exp, R=[r, dv], W=[a2], scale=dv[:, 2 + j:3 + j])
                ts(a2[:, 0:TW], a2[:, 0:TW], -1.0, 1.0, ALU.mult, ALU.add, R=[a2], W=[a2])
                ts(a2[:, 0:TW], a2[:, 0:TW], 1e-30, None, ALU.max, None, R=[a2], W=[a2])
                act(a2[:, 0:TW], a2[:, 0:TW], AF.Ln, R=[a2], W=[a2])
                act(a2[:, 0:TW], a2[:, 0:TW], AF.Exp, R=[a2], W=[a2], scale=0.5)
                act(r[:, 0:TW], r[:, 0:TW], AF.Exp, R=[r, dv], W=[r], scale=dv[:, j:j + 1])
                tt(a2[:, 0:TW], a2[:, 0:TW], ig[:, 0:TW], ALU.mult, R=[a2, ig], W=[a2])
                tt(a2[:, 0:TW], a2[:, 0:TW], xc[:, 0:TW], ALU.mult, R=[a2, xc], W=[a2])
                k.op("dve", lambda e, o=ig[:, 0:TW], d0=r[:, 0:TW], d1=a2[:, 0:TW], ini=lruh[:, j:j + 1]:
                     e.tensor_tensor_scan(o, d0, d1, ini, ALU.mult, ALU.add), R=[r, a2, lruh], W=[ig])
                cp(lruh[:, j:j + 1], ig[:, TW - 1:TW], R=[ig], W=[lruh])
                yield
                gsq = TF()
                tt(gsq[:, 0:TW], lrug[:, j, 0:TW], lrug[:, j, 0:TW], ALU.mult, R=[lrug], W=[gsq])
                ts(gsq[:, 0:TW], gsq[:, 0:TW], 0.044715, 1.0, ALU.mult, ALU.add, R=[gsq], W=[gsq])
                tt(gsq[:, 0:TW], gsq[:, 0:TW], lrug[:, j, 0:TW], ALU.mult, R=[gsq, lrug], W=[gsq])
                sigmoid_to(gsq[:, 0:TW], gsq, gsq[:, 0:TW], [gsq], 128, TW, scale=1.5957691216057308, mul_ap=lrug[:, j, 0:TW], mul_R=[lrug])
                tt(lrug[:, j, 0:TW], ig[:, 0:TW], gsq[:, 0:TW], ALU.mult, R=[ig, gsq], W=[lrug])
            yield
            p = PA
            for j in range(2):
                sq = TF()
                act(sq[:, 0:TW], lrug[:, j, 0:TW], AF.Square, R=[lrug], W=[sq])
                mm(p[:, 0:TW], ones, sq[:, 0:TW], j == 0, j == 1, R=[cst, sq], W=[p])
            r1 = TF()
            act(r1[:, 0:TW], p[:, 0:TW], AF.Ln, R=[p, cst], W=[r1], scale=1.0 / 256.0, bias=epsc)
            r2 = TF()
            act(r2[:, 0:TW], r1[:, 0:TW], AF.Exp, R=[r1], W=[r2], scale=-0.5)
            for j in range(2):
                stt(mixT[:, 4 + j, 0:TW], lrug[:, j, 0:TW], pl[:, PP["goc"] + j:PP["goc"] + j + 1], r2[:, 0:TW], ALU.mult, ALU.mult,
                    R=[lrug, pl, r2], W=ATS)

            for j in range(2):
                ysj = TF()
                conv_fm(prod, j, TW, 2, PP["scw"] + 3 * j, None, l, ysj[:, 0:TW], ysj, eng="dve")
                cp(prod[:, j, 0:2], prod[:, j, TW:TW + 2], R=[prod], W=[prod], eng="pool")
                tt(scb[:, j, 0:TW], ysj[:, 0:TW], scb[:, j, 0:TW], ALU.mult, R=[ysj, scb], W=[scb])
                yield
            p = PB
            for j in range(2):
                sq = TF()
                act(sq[:, 0:TW], scb[:, j, 0:TW], AF.Square, R=[scb], W=[sq])
                mm(p[:, 0:TW], ones, sq[:, 0:TW], j == 0, j == 1, R=[cst, sq], W=[p])
            r1 = TF()
            act(r1[:, 0:TW], p[:, 0:TW], AF.Ln, R=[p, cst], W=[r1], scale=1.0 / 256.0, bias=epsc)
            r2 = TF()
            act(r2[:, 0:TW], r1[:, 0:TW], AF.Exp, R=[r1], W=[r2], scale=-0.5)
            for j in range(2):
                stt(mixT[:, 6 + j, 0:TW], scb[:, j, 0:TW], pl[:, PP["god"] + j:PP["god"] + j + 1], r2[:, 0:TW], ALU.mult, ALU.mult,
                    R=[scb, pl, r2], W=ATS)


        ga = attention_gen(l, g, i)
        gm = mixers_gen()
        kq0_ = g["PAST"] + i * TW
        ua = 2 * (kq0_ // 512 + 1) * 9 + 8
        um = 6 + NSUB * 6 + 8
        ratio = um / float(ua)
        acc = 0.0
        a_done = m_done = False
        while not (a_done and m_done):
            if not a_done:
                try:
                    next(ga)
                except StopIteration:
                    a_done = True
            acc += ratio if not a_done else 1.0
            while acc >= 1.0 and not m_done:
                acc -= 1.0
                try:
                    next(gm)
                except StopIteration:
                    m_done = True
            if m_done:
                acc = 0.0
        for s in range(NSUB):
            group_norm_tok(l, View(otok, otok[:, s, :]), SW, s, 0, RW["goa"])

        if debug and l == 0 and n == "p" and i == 0:
            dma(dbg_mix[:, :, 0:TW], mixT[:, :, 0:TW], ATS[0], R=ATS)
        wo = [load_piece(l, 6 + hh) for hh in range(2)]
        for s in range(NSUB):
            for hh in range(2):
                slot, wv = wo[hh]
                p = PSN()
                for kk in range(8):
                    mm(p[0:SW, :], mixT[:, kk, s * SW:(s + 1) * SW], wv[:, kk, :], kk == 0, kk == 7, R=[ATS[s], slot], W=[p])
                tt(xt[0:SW, s, hh * 512:(hh + 1) * 512], xt[0:SW, s, hh * 512:(hh + 1) * 512], p[0:SW, :], ALU.add,
                   R=[XS[s], p], W=[XS[s]])

        norm_T(None)
        def ffn_stage2(r, accs):
            for j in range(2):
                act(accs[j][:, 0:TW], accs[j][:, 0:TW], AF.Silu, R=[accs[j]], W=[accs[j]])
            for j in range(2):
                tt(hff[:, 2 * r + j, 0:TW], accs[j][:, 0:TW], accs[2 + j][:, 0:TW], ALU.mult, R=[accs[j], accs[2 + j]],
                   W=hff_bufs(2 * r + j))

        pending = None
        for r in range(11):
            slot, wv = load_piece(l, 8 + r)
            accs = []
            for cc in range(4):
                mi = r * 4 + cc
                p = PSN()
                for kk in range(8):
                    mm(p[:, 0:TW], wv[:, kk, cc * 128:(cc + 1) * 128], actT[:, kk, 0:TW], kk == 0, kk == 7, R=[slot] + ATS, W=[p])
                u = tmpF[mi % 3]
                act(u[:, 2:2 + TW], p[:, 0:TW], AF.Copy, R=[p], W=[u])
                cp(u[:, 0:2], ffn_halo[:, mi, :], R=[ffn_halo], W=[u], eng="pool")
                cp(ffn_halo[:, mi, :], u[:, TW:TW + 2], R=[u], W=[ffn_halo], eng="pool")
                acc = tmpF[3 + mi % 8]
                act(acc[:, 0:TW], p[:, 0:TW], AF.Identity, R=[p, pl], W=[acc], scale=pl[:, PP["fcw"] + 3 * mi + 2:PP["fcw"] + 3 * mi + 3],
                    bias=pl[:, PP["fcb"] + mi:PP["fcb"] + mi + 1])
                stt(acc[:, 0:TW], u[:, 1:1 + TW], pl[:, PP["fcw"] + 3 * mi + 1:PP["fcw"] + 3 * mi + 2], acc[:, 0:TW], ALU.mult, ALU.add,
                    R=[u, pl, acc], W=[acc])
                stt(acc[:, 0:TW], u[:, 0:TW], pl[:, PP["fcw"] + 3 * mi:PP["fcw"] + 3 * mi + 1], acc[:, 0:TW], ALU.mult, ALU.add,
                    R=[u, pl, acc], W=[acc])
                accs.append(acc)
                if cc == 1 and pending is not None:
                    ffn_stage2(*pending)
                    pending = None
            pending = (r, accs)
        ffn_stage2(*pending)
        for hh in range(2):
            pss = [PSN() for s in range(NSUB)]
            for kg, (k0, nk) in enumerate(((0, 8), (8, 8), (16, 6))):
                slot, wv = load_piece(l, 19 + hh * 3 + kg)
                for s in range(NSUB):
                    for kk in range(nk):
                        kf = k0 + kk
                        mm(pss[s][0:SW, :], hff[:, kf, s * SW:(s + 1) * SW], wv[:, kk, :], kf == 0, kf == 21, R=hff_bufs(kf) + [slot], W=[pss[s]])
            for s in range(NSUB):
                tt(xt[0:SW, s, hh * 512:(hh + 1) * 512], xt[0:SW, s, hh * 512:(hh + 1) * 512], pss[s][0:SW, :], ALU.add,
                   R=[XS[s], pss[s]], W=[XS[s]])
                if hh == 1:
                    dma(xdst[r0 + s * SW:r0 + (s + 1) * SW, :], xt[0:SW, s, :], XS[s], R=[XS[s]])

    for l in range(DEPTH):
        prep_params(l)
        k.barrier()
        for g in groups:
            n = g["name"]
            TW, SW = g["TW"], g["SW"]
            dma(xbc_in[:, :, 0:3], g["ssm_conv0"][l], xbc_in, W=[xbc_in])
            dma(hT[:, :].rearrange("p (h c) -> p h c", h=4), g["ssm0"][l], hT, W=[hT])
            dma(lrux_in[:, :, 0:3], g["lru_conv0"][l], lrux_in, W=[lrux_in])
            dma(lruh[:, :], g["lru0"][l], lruh, W=[lruh])
            dma(prod[:, :, 0:2], g["sc_conv0"][l], prod, W=[prod])
            dma(ffn_halo[:, :, :], g["ffn_conv0"][l], ffn_halo, W=[ffn_halo])
            cp(hTb[:, :], hT[:, :], R=[hT], W=[hTb])
            for vi in range(2):
                mset(VA[vi][:, :, :, :], 1.0, W=[VA[vi]])
            if g["PAST"]:
                for b in range(g["PAST"] // 128):
                    cs = TF()
                    dma(cs[:, 0:128], g["ckv_past"][l, b * 128:(b + 1) * 128, :], cs, W=[cs])
                    ks = TF()
                    dma(ks[:, 0:32], g["kr_past"][l, b * 128:(b + 1) * 128, :], ks, W=[ks])
                    make_keys(l, 128, cs[:, 0:128], ks[:, 0:32], [cs, ks], b * 128)
            for i in range(g["NT"]):
                tile(l, g, i)
            dma(g["ssm_conv_o"][l], xbc_in[:, :, 0:3], xbc_in, R=[xbc_in])
            dma(g["ssm_o"][l], hT[:, :].rearrange("p (h c) -> p h c", h=4), hT, R=[hT])
            dma(g["lru_conv_o"][l], lrux_in[:, :, 0:3], lrux_in, R=[lrux_in])
            dma(g["lru_o"][l], lruh[:, :], lruh, R=[lruh])
            dma(g["sc_conv_o"][l], prod[:, :, 0:2], prod, R=[prod])
            dma(g["ffn_conv_o"][l], ffn_halo[:, :, :], ffn_halo, R=[ffn_halo])
        k.barrier()
    k.barrier()
    k.emit()
    st.close()
    return nc


_CACHE = {}


def make_in_maps(inp, TP):
    consts = make_consts()
    lay = [prep_layer_params(inp, l) for l in range(DEPTH)]
    rope_p = rope_table(np.arange(TP))
    rope_s = rope_table(PAST + np.arange(DEC_SEQ))
    maps = []
    for c in range(8):
        m = {"consts": consts, "rope_p": rope_p, "rope_s": rope_s}
        for l in range(DEPTH):
            for kk, v in lay[l].items():
                m["%s%d" % (kk, l)] = v
        m["x_p"] = np.ascontiguousarray(inp["x_prompt"][c % BATCH, :TP])
        m["x_s"] = np.ascontiguousarray(inp["x_sample"][c])
        m["ssm_conv0_p"] = np.zeros((DEPTH, 128, 6, 3), np.float32)
        m["ssm0_p"] = np.zeros((DEPTH, 128, 4, 64), np.float32)
        m["lru_conv0_p"] = np.zeros((DEPTH, 128, 2, 3), np.float32)
        m["lru0_p"] = np.zeros((DEPTH, 128, 2), np.float32)
        m["sc_conv0_p"] = np.zeros((DEPTH, 128, 2, 2), np.float32)
        m["ffn_conv0_p"] = np.zeros((DEPTH, 128, NFF, 2), np.float32)
        sc = inp["state_ssm_conv"][:, c]
        m["ssm_conv0_s"] = np.ascontiguousarray(sc.reshape(DEPTH, 3, 6, 128).transpose(0, 3, 2, 1))
        m["ssm0_s"] = np.ascontiguousarray(inp["state_ssm"][:, c].transpose(0, 3, 1, 2))
        lc = inp["state_lru_conv"][:, c]
        m["lru_conv0_s"] = np.ascontiguousarray(lc.reshape(DEPTH, 3, 2, 128).transpose(0, 3, 2, 1))
        m["lru0_s"] = np.ascontiguousarray(inp["state_lru"][:, c].reshape(DEPTH, 2, 128).transpose(0, 2, 1))
        s2 = inp["state_sc_conv"][:, c]
        m["sc_conv0_s"] = np.ascontiguousarray(s2.reshape(DEPTH, 2, 2, 128).transpose(0, 3, 2, 1))
        fc = inp["state_ffn_conv"][:, c][:, :, UP_PERM]
        m["ffn_conv0_s"] = np.ascontiguousarray(fc.reshape(DEPTH, 2, NFF, 128).transpose(0, 3, 2, 1))
        m["ckv_past_s"] = np.ascontiguousarray(inp["cache_mla_ckv"][:, c])
        m["kr_past_s"] = np.ascontiguousarray(inp["cache_mla_krope"][:, c])
        maps.append(m)
    return maps


INV_UP = np.argsort(UP_PERM)


def unpack_group(res, n):
    y = np.stack([r["y_" + n] for r in res])
    ckv = np.stack([r["ckv_o_" + n] for r in res], axis=1)
    kr = np.stack([r["kr_o_" + n] for r in res], axis=1)
    ssm_conv = np.stack([r["ssm_conv_o_" + n].transpose(0, 3, 2, 1).reshape(DEPTH, 3, 768) for r in res], axis=1)
    ssm = np.stack([r["ssm_o_" + n].transpose(0, 2, 3, 1) for r in res], axis=1)
    lru_conv = np.stack([r["lru_conv_o_" + n].transpose(0, 3, 2, 1).reshape(DEPTH, 3, 256) for r in res], axis=1)
    lru = np.stack([r["lru_o_" + n].transpose(0, 2, 1).reshape(DEPTH, 256) for r in res], axis=1)
    sc_conv = np.stack([r["sc_conv_o_" + n].transpose(0, 3, 2, 1).reshape(DEPTH, 2, 256) for r in res], axis=1)
    ffn_conv = np.stack([r["ffn_conv_o_" + n].transpose(0, 3, 2, 1).reshape(DEPTH, 2, 2 * DFF)[:, :, INV_UP] for r in res], axis=1)
    return [np.ascontiguousarray(a.astype(np.float32)) for a in (y, ckv, kr, ssm_conv, ssm, lru_conv, lru, sc_conv, ffn_conv)]


def run(inp, TP=SEQ):
    inp = {kk: np.asarray(v) for kk, v in inp.items()}
    if TP not in _CACHE:
        _CACHE[TP] = build(TP)
    nc = _CACHE[TP]
    maps = make_in_maps(inp, TP)
    res = run_bass_kernel_spmd(nc, maps, core_ids=list(range(8)))
    R = res.results
    p = unpack_group([R[b] for b in range(BATCH)], "p")
    s = unpack_group([R[c] for c in range(DEC_BATCH)], "s")
    return (p[0], s[0], *p[1:], *s[1:])


def kernel(**inputs):
    return run(inputs, SEQ)
```

```python
import contextlib
import numpy as np
import concourse.bass as bass
import concourse.mybir as mybir
from concourse.bass_utils import run_bass_kernel_spmd

F32 = mybir.dt.float32
BF16 = mybir.dt.bfloat16
AF = mybir.ActivationFunctionType
ALU = mybir.AluOpType
AX = mybir.AxisListType

D = 1024
DEPTH = 2
SEQ = 16384
BATCH = 2
DEC_BATCH = 8
DEC_SEQ = 64
PAST = 2048
DFF = 2816
EPS = 1e-6
NPIECE = 25
NFF = 44
ENG = ("pe", "act", "dve", "pool", "sp")


class Buf:
    __slots__ = ("name", "t", "w", "rd", "dsem", "dcnt")

    def __init__(self, name, t):
        self.name = name
        self.t = t
        self.w = None
        self.rd = {}
        self.dsem = None
        self.dcnt = 0

    def __getitem__(self, key):
        return self.t[key]


class BufGroup:
    def __init__(self, name, t, n):
        self.t = t
        self.subs = [Buf("%s_%d" % (name, i), t) for i in range(n)]

    def __getitem__(self, key):
        return self.t[key]

    def s(self, i):
        return self.subs[i]


def _flat(bs):
    out = []
    for b in bs:
        b = getattr(b, "base", b)
        if isinstance(b, BufGroup):
            out.extend(b.subs)
        else:
            out.append(b)
    return out


class View:
    def __init__(self, base, ap):
        self.base = base
        self.t = ap

    def __getitem__(self, key):
        return self.t[key]


class Op:
    __slots__ = ("fn", "waits", "signal", "dbuf")

    def __init__(self, fn, waits, dbuf):
        self.fn = fn
        self.waits = waits
        self.signal = False
        self.dbuf = dbuf


class K:
    def __init__(self, nc):
        self.nc = nc
        self.q = {e: [] for e in ENG}
        self.waited = {e: {f: -1 for f in ENG} for e in ENG}
        self.waited_d = {e: {} for e in ENG}
        self.dbufs = []
        self.same_raw = True

    def _dep_eng(self, eng, dep, waits, same_ok):
        f, idx = dep
        if f == eng:
            if eng == "pe" or not same_ok:
                return
        if self.waited[eng][f] >= idx:
            return
        self.waited[eng][f] = idx
        self.q[f][idx].signal = True
        waits.append(("e", f, idx))

    def _dep_dma(self, eng, b, waits):
        if b.dcnt == 0:
            return
        if self.waited_d[eng].get(id(b), 0) >= b.dcnt:
            return
        self.waited_d[eng][id(b)] = b.dcnt
        waits.append(("d", b, b.dcnt))

    def op(self, eng, fn, R=(), W=(), dma=None):
        R = _flat(R)
        W = _flat(W)
        if dma is not None:
            dma = getattr(dma, "base", dma)
            if isinstance(dma, BufGroup):
                dma = dma.subs[0]
        waits = []
        for b in R:
            if b.w is not None:
                self._dep_eng(eng, b.w, waits, self.same_raw)
            self._dep_dma(eng, b, waits)
        for b in W:
            if b.w is not None:
                self._dep_eng(eng, b.w, waits, False)
            for f, idx in b.rd.items():
                self._dep_eng(eng, (f, idx), waits, False)
            self._dep_dma(eng, b, waits)
        idx = len(self.q[eng])
        o = Op(fn, waits, dma)
        self.q[eng].append(o)
        if dma is not None:
            if dma.dsem is None:
                dma.dsem = len(self.dbufs)
                self.dbufs.append(dma)
            dma.dcnt += 1
        else:
            for b in R:
                b.rd[eng] = idx
            for b in W:
                b.w = (eng, idx)
                b.rd = {}
        return idx

    def barrier(self):
        last = {}
        for f in ENG:
            if f == "sp":
                continue
            for i in range(len(self.q[f]) - 1, -1, -1):
                if self.q[f][i].fn is not None:
                    last[f] = i
                    break
        for e in ENG:
            waits = []
            for f, i in last.items():
                if f == e:
                    continue
                self._dep_eng(e, (f, i), waits, False)
            for b in self.dbufs:
                self._dep_dma(e, b, waits)
            self.q[e].append(Op(None, waits, None))

    def emit(self):
        nc = self.nc
        with contextlib.ExitStack() as st:
            esem = {e: st.enter_context(nc.semaphore("sem_" + e)) for e in ENG}
            dsem = [st.enter_context(nc.semaphore("dsem%d" % i)) for i in range(len(self.dbufs))]
            sval = {}
            for e in ENG:
                c = 0
                for i, o in enumerate(self.q[e]):
                    if o.signal:
                        c += 1
                        sval[(e, i)] = c
            block = st.enter_context(nc.Block())

            def run(e, engine):
                dcount = {}
                for i, o in enumerate(self.q[e]):
                    for wt in o.waits:
                        if wt[0] == "e":
                            engine.wait_ge(esem[wt[1]], sval[(wt[1], wt[2])])
                        else:
                            engine.wait_ge(dsem[wt[1].dsem], 16 * wt[2])
                    if o.fn is None:
                        continue
                    ins = o.fn(engine)
                    if o.dbuf is not None:
                        ins.then_inc(dsem[o.dbuf.dsem], 16)
                    elif o.signal:
                        ins.then_inc(esem[e], 1)

            @block.tensor
            def _(engine):
                run("pe", engine)

            @block.scalar
            def _(engine):
                run("act", engine)

            @block.vector
            def _(engine):
                run("dve", engine)

            @block.gpsimd
            def _(engine):
                run("pool", engine)

            @block.sync
            def _(engine):
                run("sp", engine)


NCONST = 1028


def make_consts():
    c = np.zeros((128, NCONST), np.float32)
    k = np.arange(128)
    c[:, 0:128] = np.eye(128, dtype=np.float32)
    same = (k[:, None] // 64) == (k[None, :] // 64)
    c[:, 128:256] = (same & (k[:, None] > k[None, :])).astype(np.float32)
    c[:, 256:384] = (same & (k[:, None] <= k[None, :])).astype(np.float32)
    c[:, 384:512] = same.astype(np.float32)
    t = np.arange(64)
    c[:, 512:576] = ((k[:, None] % 64) <= t[None, :]).astype(np.float32)
    c[:, 576:640] = (t[None, :] >= (k[:, None] % 64)).astype(np.float32)
    sel = np.zeros((128, 2, 128), np.float32)
    sel[0:64, 0, :] = 1.0
    sel[64:128, 1, :] = 1.0
    c[:, 640:896] = sel.reshape(128, 256)
    c[:, 896:1024] = 1.0
    c[:, 1024] = EPS
    c[:, 1025] = -0.5 * np.log(96.0)
    return c


def rope_table(pos):
    half = 16
    inv = (10000.0 ** (-np.arange(half, dtype=np.float32) / half)).astype(np.float32)
    ang = pos.astype(np.float32)[:, None] * inv[None, :]
    cos, sin = np.cos(ang).astype(np.float32), np.sin(ang).astype(np.float32)
    return np.concatenate([cos, cos, -sin, sin], axis=1).astype(np.float32)


W_IN_PERM = np.concatenate([np.arange(0, 416), np.arange(1440, 1444), np.arange(416, 672),
                            np.arange(672, 1440), np.arange(1444, 2724)])
UP_PERM = np.concatenate([np.concatenate([np.arange(256 * r, 256 * r + 256),
                                          DFF + np.arange(256 * r, 256 * r + 256)]) for r in range(11)])
NROW = 652
NPP = 250
PP = dict(gm=0, gf=8, gqa=16, cws=18, cbs=42, lcw=48, lcb=56, ba=58, bx=60, lam=62, scw=64,
          goc=70, god=72, fcw=74, fcb=206)
RW = dict(gkv=0, dtb=128, alog=132, dd=136, goa=140, gob=396)


def chan(v, n):
    return np.ascontiguousarray(v.reshape(n, 128).T)


def prep_layer_params(inp, l):
    pp = np.zeros((128, NPP), np.float32)
    pp[:, 0:8] = chan(inp["ln_mix_g"][l], 8)
    pp[:, 8:16] = chan(inp["ln_ffn_g"][l], 8)
    pp[:, 16:18] = chan(inp["q_a_norm_g"][l], 2)
    cw = inp["ssm_conv_w"][l]
    pp[:, 18:42] = np.stack([chan(cw[k], 6) for k in range(4)], axis=2).reshape(128, 24)
    pp[:, 42:48] = chan(inp["ssm_conv_b"][l], 6)
    lw = inp["lru_conv_w"][l]
    pp[:, 48:56] = np.stack([chan(lw[k], 2) for k in range(4)], axis=2).reshape(128, 8)
    pp[:, 56:58] = chan(inp["lru_conv_b"][l], 2)
    pp[:, 58:60] = chan(inp["lru_b_a"][l], 2)
    pp[:, 60:62] = chan(inp["lru_b_x"][l], 2)
    pp[:, 62:64] = chan(inp["lru_lambda"][l], 2)
    sw = inp["sc_conv_w"][l]
    pp[:, 64:70] = np.stack([chan(sw[k], 2) for k in range(3)], axis=2).reshape(128, 6)
    og = inp["out_norm_g"][l]
    pp[:, 70:72] = chan(og[512:768], 2)
    pp[:, 72:74] = chan(og[768:1024], 2)
    fw = inp["ffn_conv_w"][l][:, UP_PERM]
    pp[:, 74:206] = np.stack([chan(fw[k], NFF) for k in range(3)], axis=2).reshape(128, 132)
    pp[:, 206:250] = chan(inp["ffn_conv_b"][l][UP_PERM], NFF)

    rows = np.zeros((NROW,), np.float32)
    rows[0:128] = inp["kv_a_norm_g"][l]
    gq, gk = inp["q_norm_g"][l], inp["k_norm_g"][l]
    gq_r, gk_r = gq, gk
    rows[128:132] = inp["ssm_dt_bias"][l]
    rows[132:136] = inp["ssm_a_log"][l]
    rows[136:140] = inp["ssm_d"][l]
    rows[140:396] = og[0:256]
    rows[396:652] = og[256:512]
    rows = np.ascontiguousarray(np.broadcast_to(rows[None, :], (128, NROW)))
    gqg = np.concatenate([np.tile(gq_r, 4), np.tile(gk_r, 4)])
    gqg = np.ascontiguousarray(np.broadcast_to(gqg[None, :], (128, 768)))

    bd = np.zeros((128, 4, 128), np.float32)
    for j in range(2):
        for q in range(2):
            bd[q * 64:(q + 1) * 64, j, q * 64:(q + 1) * 64] = inp["lru_w_a"][l][2 * j + q]
            bd[q * 64:(q + 1) * 64, 2 + j, q * 64:(q + 1) * 64] = inp["lru_w_x"][l][2 * j + q]
    wqb = np.ascontiguousarray(inp["w_q_b"][l].reshape(2, 128, 384).transpose(1, 0, 2))
    wkvb = inp["w_kv_b"][l].reshape(128, 4, 128)
    wkv = np.zeros((128, 512), np.float32)
    wkv[:, 0:256] = wkvb[:, :, 0:64].reshape(128, 256)
    wkv[:, 256:512] = wkvb[:, :, 64:128].reshape(128, 256)
    win = np.ascontiguousarray(inp["w_in"][l][:, W_IN_PERM].reshape(8, 128, 2724))
    wout = np.ascontiguousarray(inp["w_out"][l].reshape(8, 128, 1024))
    wup = np.ascontiguousarray(inp["w_ffn_up"][l][:, UP_PERM].reshape(8, 128, 5632))
    wdn = np.ascontiguousarray(inp["w_ffn_down"][l].reshape(22, 128, 1024))
    return dict(pp=pp, rows=rows, gqg=gqg, bd=bd, wqb=wqb, wkv=wkv, win=win, wout=wout, wup=wup, wdn=wdn)


def piece_table():
    P = [("win", 0, 8, 0, 420), ("win", 0, 8, 420, 256)]
    for i in range(4):
        P.append(("win", 0, 8, 676 + 512 * i, 512))
    for i in range(2):
        P.append(("wout", 0, 8, 512 * i, 512))
    for r in range(11):
        P.append(("wup", 0, 8, 512 * r, 512))
    for hh in range(2):
        for (k0, nk) in ((0, 8), (8, 8), (16, 6)):
            P.append(("wdn", k0, nk, 512 * hh, 512))
    return P


PIECES = piece_table()


def build(TP=SEQ, debug=False):
    nc = bass.Bass("TRN2", target_bir_lowering=False)
    k = K(nc)
    st = contextlib.ExitStack()

    def din(name, shape, dt=F32):
        return nc.dram_tensor(name, list(shape), dt, kind="ExternalInput").ap()

    def dout(name, shape, dt=F32):
        return nc.dram_tensor(name, list(shape), dt, kind="ExternalOutput").ap()

    def dscr(name, shape, dt=F32):
        return nc.dram_tensor(name, list(shape), dt, kind="Internal").ap()

    groups = [dict(name="p", T=TP, PAST=0, TW=min(512, TP)), dict(name="s", T=DEC_SEQ, PAST=PAST, TW=64)]
    for g in groups:
        g["SW"] = min(128, g["TW"])
        g["NSUB"] = g["TW"] // g["SW"]
        g["NT"] = g["T"] // g["TW"]
        g["NK"] = g["PAST"] + g["T"]
        n = g["name"]
        g["x"] = din("x_" + n, (g["T"], D))
        g["rope"] = din("rope_" + n, (g["T"], 64))
        g["ssm_conv0"] = din("ssm_conv0_" + n, (DEPTH, 128, 6, 3))
        g["ssm0"] = din("ssm0_" + n, (DEPTH, 128, 4, 64))
        g["lru_conv0"] = din("lru_conv0_" + n, (DEPTH, 128, 2, 3))
        g["lru0"] = din("lru0_" + n, (DEPTH, 128, 2))
        g["sc_conv0"] = din("sc_conv0_" + n, (DEPTH, 128, 2, 2))
        g["ffn_conv0"] = din("ffn_conv0_" + n, (DEPTH, 128, NFF, 2))
        if g["PAST"]:
            g["ckv_past"] = din("ckv_past_" + n, (DEPTH, g["PAST"], 128))
            g["kr_past"] = din("kr_past_" + n, (DEPTH, g["PAST"], 32))
        g["y"] = dout("y_" + n, (g["T"], D))
        g["ckv_o"] = dout("ckv_o_" + n, (DEPTH, g["T"], 128))
        g["kr_o"] = dout("kr_o_" + n, (DEPTH, g["T"], 32))
        g["ssm_conv_o"] = dout("ssm_conv_o_" + n, (DEPTH, 128, 6, 3))
        g["ssm_o"] = dout("ssm_o_" + n, (DEPTH, 128, 4, 64))
        g["lru_conv_o"] = dout("lru_conv_o_" + n, (DEPTH, 128, 2, 3))
        g["lru_o"] = dout("lru_o_" + n, (DEPTH, 128, 2))
        g["sc_conv_o"] = dout("sc_conv_o_" + n, (DEPTH, 128, 2, 2))
        g["ffn_conv_o"] = dout("ffn_conv_o_" + n, (DEPTH, 128, NFF, 2))
        g["x1"] = dscr("x1_" + n, (g["T"], D))
    consts_d = din("consts", (128, NCONST))
    if debug:
        dbg_mix = dout("dbg_mix", (128, 8, 512), BF16)
    L = []
    for l in range(DEPTH):
        L.append(dict(
            pp=din("pp%d" % l, (128, NPP)), rows=din("rows%d" % l, (128, NROW)), gqg=din("gqg%d" % l, (128, 768)),
            bd=din("bd%d" % l, (128, 4, 128)), wqb=din("wqb%d" % l, (128, 2, 384)),
            wkv=din("wkv%d" % l, (128, 512)), win=din("win%d" % l, (8, 128, 2724)),
            wout=din("wout%d" % l, (8, 128, 1024)), wup=din("wup%d" % l, (8, 128, 5632)),
            wdn=din("wdn%d" % l, (22, 128, 1024)),
            wscr=dscr("wscr%d" % l, (NPIECE, 128, 4096), BF16)))

    NKMAX = max(g["NK"] for g in groups)
    NBLK = (NKMAX + 127) // 128

    def sb(name, shape, dt=F32):
        return Buf(name, st.enter_context(nc.sbuf_tensor("sb_" + name, list(shape), dt)))

    def psb(name):
        return Buf(name, st.enter_context(nc.psum_tensor(name, [128, 512], F32)))

    PS = [psb("ps%d" % i) for i in range(8)]
    cst = sb("cst", (128, NCONST))
    identb = sb("identb", (128, 128), BF16)
    ppL = [sb("ppS", (128, NPP))] * DEPTH
    rowsL = [sb("rowsS", (128, NROW))] * DEPTH
    dervL = [sb("dervS", (128, 16))] * DEPTH
    gqkL = [sb("gqkS", (128, 384))] * DEPTH
    bdL = [sb("bdS", (128, 4, 128), BF16)] * DEPTH
    wqbL = [sb("wqbS", (128, 2, 384), BF16)] * DEPTH
    wkvL = [sb("wkvS", (128, 512), BF16)] * DEPTH
    ckvT = sb("ckvT", (128, NKMAX), BF16)
    krtok = sb("krtok", (128, NBLK, 32), BF16)
    skt = sb("skt", (128, NBLK, 4))
    xt = sb("xt0", (128, 4, D))
    xts = [xt, xt]
    XS = [Buf("xs%d" % i, xt.t) for i in range(4)]
    actT = sb("actT", (128, 8, 512), BF16)
    mixT = actT
    ATS = [Buf("actT_s%d" % i, actT[:, :, i * 128:(i + 1) * 128]) for i in range(4)]
    NSLOT = 3
    ring = [sb("ring%d" % i, (128, 4096), BF16) for i in range(NSLOT)]
    xbc_in = BufGroup("xbc_in", sb("xbc_in", (128, 6, 516)).t, 6)
    lrux_in = BufGroup("lrux_in", sb("lrux_in", (128, 2, 516)).t, 2)
    prod = BufGroup("prod", sb("prod", (128, 2, 516)).t, 2)
    ffn_halo = sb("ffn_halo", (128, NFF, 2))
    hT = sb("hT", (128, 256))
    hTb = sb("hTb", (128, 256), BF16)
    lruh = sb("lruh", (128, 2))
    U = sb("U", (128, 11264), BF16)
    hff = Buf("hff", U[:, :].rearrange("p (k c) -> p k c", k=22))
    xbc_c = BufGroup("xbc_c", U[:, 0:3072].rearrange("p (c t) -> p c t", c=6), 6)
    lrug = BufGroup("lrug", U[:, 3072:5120].bitcast(F32).rearrange("p (j t) -> p j t", j=2), 2)
    scb = BufGroup("scb", U[:, 5120:7168].bitcast(F32).rearrange("p (j t) -> p j t", j=2), 2)
    szt = Buf("szt", U[:, 7168:9216].bitcast(F32).rearrange("p (s c) -> p s c", s=4))
    QT = Buf("QT", U[0:96, 9216:11264].rearrange("p (h t) -> p h t", h=4))
    HFK = [Buf("hffk%d" % i, hff.t) for i in range(3)]

    def hff_bufs(m):
        al = xbc_c.s(m) if m < 6 else lrug.s((m - 6) // 2) if m < 10 else scb.s((m - 10) // 2) if m < 14 else szt if m < 18 else QT
        return [HFK[m // 8], al]
    otok = sb("otok", (128, 4, 256))
    ATF = [sb("atf%d" % i, (128, 516)) for i in range(3)]
    ASF = [sb("asf%d" % i, (128, 16)) for i in range(4)]
    KT = [sb("KT%d" % i, (96, 512), BF16) for i in range(2)]
    VA = [sb("VA%d" % i, (128, 4, 4, 65), BF16) for i in range(2)]
    KRT = [sb("KRT%d" % i, (96, 512), BF16) for i in range(2)]
    NPT = 6
    LOOKAHEAD = 4
    PT = [sb("PT%d" % i, (128, 512), BF16) for i in range(NPT)]
    dtt = sb("dtt", (128, 4, 4))
    ckv_st = sb("ckv_st", (128, 4, 128))
    kr_st = sb("kr_st", (128, 4, 32))
    ropet0 = sb("ropet0", (128, 4, 64))
    ropets = [ropet0, ropet0]
    xstate = {"n": 0, "pre": None}
    NTMP = 11
    tmpF = [sb("tmpF%d" % i, (128, 516)) for i in range(NTMP)]
    tmpB = [sb("tmpB%d" % i, (128, 1024), BF16) for i in range(4)]
    smallF = [sb("smallF%d" % i, (128, 16)) for i in range(12)]
    cnt = {"f": 0, "b": 0, "s": 0, "ps": 0, "ring": 0, "pt": 0}

    def TF():
        cnt["f"] += 1
        return tmpF[cnt["f"] % NTMP]

    def TB():
        cnt["b"] += 1
        return tmpB[cnt["b"] % 4]

    def SF():
        cnt["s"] += 1
        return smallF[cnt["s"] % 12]

    ps_pool = list(range(8))

    def PSN():
        cnt["ps"] += 1
        return PS[ps_pool[cnt["ps"] % len(ps_pool)]]

    def dma(out, in_, buf, R=(), W=(), eng="sp"):
        k.op(eng, lambda e: e.dma_start(out=out, in_=in_), R=R, W=W, dma=buf)

    def mm(out, lhsT, rhs, start, stop, R, W):
        k.op("pe", lambda e: e.matmul(out, lhsT, rhs, start=start, stop=stop), R=R, W=W)

    def tr(out, in_, ident, R, W):
        k.op("pe", lambda e: e.transpose(out, in_, ident), R=R, W=W)

    def act(out, in_, func, R, W, bias=None, scale=None, accum=None, eng="act"):
        kw = {}
        if bias is not None:
            kw["bias"] = bias
        if scale is not None:
            kw["scale"] = scale
        if accum is not None:
            kw["accum_out"] = accum
        k.op(eng, lambda e: e.activation(out=out, in_=in_, func=func, **kw), R=R, W=W)

    def ts(out, in0, s1, s2, op0, op1, R, W, eng="dve"):
        if op1 is None:
            k.op(eng, lambda e: e.tensor_scalar(out=out, in0=in0, scalar1=s1, scalar2=None, op0=op0), R=R, W=W)
        else:
            k.op(eng, lambda e: e.tensor_scalar(out=out, in0=in0, scalar1=s1, scalar2=s2, op0=op0, op1=op1), R=R, W=W)

    def tt(out, in0, in1, op, R, W, eng="dve"):
        k.op(eng, lambda e: e.tensor_tensor(out=out, in0=in0, in1=in1, op=op), R=R, W=W)

    def stt(out, in0, scalar, in1, op0, op1, R, W, eng="dve"):
        k.op(eng, lambda e: e.scalar_tensor_tensor(out=out, in0=in0, scalar=scalar, in1=in1, op0=op0, op1=op1), R=R, W=W)

    def cp(out, in_, R, W, eng="dve"):
        k.op(eng, lambda e: e.tensor_copy(out=out, in_=in_), R=R, W=W)

    def red(out, in_, R, W):
        k.op("dve", lambda e: e.tensor_reduce(out=out, in_=in_, axis=AX.X, op=ALU.add), R=R, W=W)

    def recip(out, in_, R, W):
        k.op("dve", lambda e: e.reciprocal(out=out, in_=in_), R=R, W=W)

    def mset(ap, val, W, eng="pool"):
        k.op(eng, lambda e: e.memset(ap, val), W=W)

    ident = cst[:, 0:128]
    U2 = cst[:, 128:256]
    T2 = cst[:, 256:384]
    O2 = cst[:, 384:512]
    tri2 = cst[:, 512:576]
    mask01 = cst[:, 576:640]
    ones = cst[:, 896:1024]
    epsc = cst[:, 1024:1025]
    LN96H = cst[:, 1025:1026]

    def rstd_from(ms, n, R):
        s1 = SF()
        act(s1[0:n, 0:1], ms, AF.Ln, R=list(R) + [cst], W=[s1], bias=epsc[0:n, :])
        s2 = SF()
        act(s2[0:n, 0:1], s1[0:n, 0:1], AF.Exp, R=[s1], W=[s2], scale=-0.5)
        return s2

    def sigmoid_to(out_ap, out_buf, in_ap, Rin, n, w, scale=1.0, nbias=None, mul_ap=None, mul_R=()):
        e = TF()
        if nbias is not None:
            act(e[0:n, 0:w], in_ap, AF.Exp, R=list(Rin), W=[e], scale=-scale, bias=nbias)
        else:
            act(e[0:n, 0:w], in_ap, AF.Exp, R=list(Rin), W=[e], scale=-scale)
        ts(e[0:n, 0:w], e[0:n, 0:w], 1.0, None, ALU.add, None, R=[e], W=[e])
        if mul_ap is None:
            recip(out_ap, e[0:n, 0:w], R=[e], W=[out_buf])
        else:
            recip(e[0:n, 0:w], e[0:n, 0:w], R=[e], W=[e])
            tt(out_ap, mul_ap, e[0:n, 0:w], ALU.mult, R=list(mul_R) + [e], W=[out_buf])

    for b_ in ATF:
        mset(b_[64:66, :], 0.0, W=[b_])
    dma(cst[:, :], consts_d[:, :], cst, W=[cst])
    cp(identb[:, :], ident, R=[cst], W=[identb])
    def prep_params(l):
        Ld = L[l]
        dma(ppL[l][:, :], Ld["pp"][:, :], ppL[l], W=[ppL[l]])
        dma(rowsL[l][:, :], Ld["rows"][:, :], rowsL[l], W=[rowsL[l]])
        gt = TF()
        gt2 = TF()
        dma(gt[:, 0:384], Ld["gqg"][:, 0:384], gt, W=[gt])
        dma(gt2[:, 0:384], Ld["gqg"][:, 384:768], gt2, W=[gt2])
        tt(gqkL[l][:, :], gt[:, 0:384], gt2[:, 0:384], ALU.mult, R=[gt, gt2], W=[gqkL[l]])
        dv = dervL[l]
        t0 = SF()
        act(t0[:, 0:2], ppL[l][:, PP["lam"]:PP["lam"] + 2], AF.Exp, R=[ppL[l]], W=[t0], scale=-1.0)
        t1 = SF()
        act(t1[:, 0:2], t0[:, 0:2], AF.Ln, R=[t0], W=[t1], bias=1.0)
        ts(dv[:, 0:2], t1[:, 0:2], -8.0, None, ALU.mult, None, R=[t1], W=[dv])
        ts(dv[:, 2:4], t1[:, 0:2], -16.0, None, ALU.mult, None, R=[t1], W=[dv])
        t2 = SF()
        act(t2[:, 0:4], rowsL[l][:, RW["alog"]:RW["alog"] + 4], AF.Exp, R=[rowsL[l]], W=[t2])
        ts(dv[:, 4:8], t2[:, 0:4], -1.0, None, ALU.mult, None, R=[t2], W=[dv])
        ts(dv[:, 8:12], ppL[l][:, PP["ba"]:PP["ba"] + 4], -1.0, None, ALU.mult, None, R=[ppL[l]], W=[dv])
        f = xt
        dma(f[:, 0, 0:512].rearrange("p (a b) -> p a b", a=4), Ld["bd"][:, :, :], f, W=[f])
        cp(bdL[l][:, :, :], f[:, 0, 0:512].rearrange("p (a b) -> p a b", a=4), R=[f], W=[bdL[l]])
        dma(f[:, 1, 0:768].rearrange("p (a b) -> p a b", a=2), Ld["wqb"][:, :, :], f, W=[f])
        for kk in range(2):
            ts(wqbL[l][:, kk, :], f[:, 1, kk * 384:(kk + 1) * 384], ppL[l][:, PP["gqa"] + kk:PP["gqa"] + kk + 1], None,
               ALU.mult, None, R=[f, ppL[l]], W=[wqbL[l]])
        dma(f[:, 2, 0:512], Ld["wkv"][:, :], f, W=[f])
        cp(wkvL[l][:, :], f[:, 2, 0:512], R=[f], W=[wkvL[l]])

    for l in range(DEPTH):
        Ld = L[l]
        prep_params(l)
        for pi, (src, k0, nk, c0, ncol) in enumerate(PIECES):
            if pi % 2 == 0:
                stg = xt
                stgf = stg[:, :, :].rearrange("p a b -> p (a b)")
            else:
                stg = hff
                stgf = U[:, 0:8192].bitcast(F32)
            stgv = stgf[:, 0:nk * ncol].rearrange("p (k c) -> p k c", k=nk)
            dma(stgv, Ld[src][k0:k0 + nk, :, c0:c0 + ncol].rearrange("k p c -> p k c"), stg, W=[stg])
            slot = ring[pi % NSLOT]
            slv = slot[:, 0:nk * ncol].rearrange("p (k c) -> p k c", k=nk)
            for kk in range(nk):
                on_act = (kk % 2 == 1)
                if src in ("win", "wup"):
                    gc = PP["gm"] if src == "win" else PP["gf"]
                    if on_act:
                        act(slv[:, kk, :], stgv[:, kk, :], AF.Identity, R=[stg, ppL[l]], W=[slot], scale=ppL[l][:, gc + kk:gc + kk + 1])
                    else:
                        ts(slv[:, kk, :], stgv[:, kk, :], ppL[l][:, gc + kk:gc + kk + 1], None, ALU.mult, None,
                           R=[stg, ppL[l]], W=[slot])
                else:
                    if on_act:
                        act(slv[:, kk, :], stgv[:, kk, :], AF.Copy, R=[stg], W=[slot])
                    else:
                        cp(slv[:, kk, :], stgv[:, kk, :], R=[stg], W=[slot])
            dma(Ld["wscr"][pi, :, 0:nk * ncol], slot[:, 0:nk * ncol], slot, R=[slot])
    k.barrier()

    def load_piece(l, pi, slot=None):
        src, k0, nk, c0, ncol = PIECES[pi]
        if slot is None:
            cnt["ring"] += 1
            slot = ring[cnt["ring"] % NSLOT]
        dma(slot[:, 0:nk * ncol], L[l]["wscr"][pi, :, 0:nk * ncol], slot, W=[slot])
        return slot, slot[:, 0:nk * ncol].rearrange("p (k c) -> p k c", k=nk)

    def make_keys(l, n, ckv_ap, kr_ap, Rb, kpos, banks=None):
        blk = kpos // 128
        r0 = kpos % 128
        cb = TB()
        cp(cb[0:n, 0:128], ckv_ap, R=Rb, W=[cb])
        p = banks[0] if banks else PSN()
        pv = p[:, :].bitcast(BF16)
        tr(pv[:, 0:n], cb[0:n, 0:128], identb[0:n, 0:n], R=[cb, identb], W=[p])
        cp(ckvT[:, kpos:kpos + n], pv[:, 0:n], R=[p], W=[ckvT])
        cp(krtok[r0:r0 + n, blk, :], kr_ap, R=Rb, W=[krtok], eng="pool")
        p2 = banks[1] if banks else PSN()
        mm(p2[0:n, 0:256], ckvT[:, kpos:kpos + n], wkvL[l][:, 0:256], True, True, R=[ckvT, wkvL[l]], W=[p2])
        sq = TF()
        act(sq[0:n, 0:256], p2[0:n, 0:256], AF.Square, R=[p2], W=[sq])
        s4 = SF()
        red(s4[0:n, 0:4], sq[0:n, 0:256].rearrange("p (h c) -> p h c", h=4), R=[sq], W=[s4])
        jk = TF()
        s1 = SF()
        act(jk[0:n, 0:32], kr_ap, AF.Square, R=Rb, W=[jk, s1], accum=s1[0:n, 0:1])
        s5 = SF()
        ts(s5[0:n, 0:4], s4[0:n, 0:4], s1[0:n, 0:1], 1.0 / 96.0, ALU.add, ALU.mult, R=[s4, s1], W=[s5])
        s6 = SF()
        act(s6[0:n, 0:4], s5[0:n, 0:4], AF.Ln, R=[s5, cst], W=[s6], bias=epsc[0:n, :])
        act(skt[r0:r0 + n, blk, :], s6[0:n, 0:4], AF.Exp, R=[s6], W=[skt], scale=-0.5, bias=LN96H[0:n, :])

    def attention_gen(l, g, i):
        TW, SW, NSUB = g["TW"], g["SW"], g["NSUB"]
        kq0 = g["PAST"] + i * TW
        nfull = kq0 // 512
        assert kq0 % 512 == 0
        O = [PS[0], PS[1]]
        ac = {"ps": 0, "tf": 0, "sf": 0}

        def APS():
            ac["ps"] += 1
            return PS[2 + ac["ps"] % 4]

        def ATFn():
            ac["tf"] += 1
            return ATF[ac["tf"] % 3]

        def ASFn():
            ac["sf"] += 1
            return ASF[ac["sf"] % 4]

        nsb_d = (TW + 127) // 128
        n_pv = nfull * 4 + sum(1 + (1 if j * 128 + 64 < TW else 0) for j in range(nsb_d))

        def build_kv(k0, kw, hp):
            nsb = (kw + 127) // 128
            sw = min(128, kw)
            va = VA[(k0 // 512) % 2]
            p = APS()
            for j in range(nsb):
                mm(p[0:sw, j * 128:(j + 1) * 128], ckvT[:, k0 + j * 128:k0 + j * 128 + sw],
                   wkvL[l][:, 256 + hp * 128:256 + (hp + 1) * 128], True, True, R=[ckvT, wkvL[l]], W=[p])
            cp(va[0:sw, 0:nsb, 0:2, 0:64], p[0:sw, 0:nsb * 128].rearrange("p (j h c) -> p j h c", j=nsb, h=2), R=[p], W=[va])
            return va

        def build_krT(k0, kw):
            nsb = (kw + 127) // 128
            p = APS()
            pv = p[:, :].bitcast(BF16)
            for j in range(nsb):
                sw = min(128, kw - j * 128)
                blk = (k0 + j * 128) // 128
                tr(pv[64:96, j * 128:j * 128 + sw], krtok[0:sw, blk, :], identb[0:sw, 0:sw], R=[krtok, identb], W=[p])
            krT = KRT[(k0 // 512) % 2]
            cp(krT[64:96, 0:kw], pv[64:96, 0:kw], R=[p], W=[krT])
            return krT

        def build_kt(k0, kw, h, krT):
            kt = KT[h % 2]
            p = APS()
            mm(p[0:64, 0:kw], wkvL[l][:, h * 64:(h + 1) * 64], ckvT[:, k0:k0 + kw], True, True, R=[ckvT, wkvL[l]], W=[p])
            cp(kt[0:64, 0:kw], p[0:64, 0:kw], R=[p], W=[kt])
            cp(kt[64:96, 0:kw], krT[64:96, 0:kw], R=[krT], W=[kt])
            return kt

        for hp in range(2):
            started = [False, False]
            done = [0, 0]
            pend = []

            def s_exp_pv(kt, va, hl, j, krow, qlo, qhi, blk):
                h = 2 * hp + hl
                nq = qhi - qlo
                p = APS()
                mm(p[0:krow, 0:nq], kt[:, j * 128:j * 128 + krow], QT[:, h, qlo:qhi], True, True, R=[kt, QT], W=[p])
                cnt["pt"] += 1
                pt = PT[cnt["pt"] % NPT]
                act(pt[0:krow, 0:nq], p[0:krow, 0:nq], AF.Exp, R=[p, skt], W=[pt], scale=skt[0:krow, blk, h:h + 1])

                def pv():
                    done[hl] += 1
                    mm(O[hl][0:65, qlo:qhi], va[0:krow, j, hl, :], pt[0:krow, 0:nq], not started[hl], done[hl] == n_pv,
                       R=[va, pt], W=[O[hl]])
                    started[hl] = True
                pend.append(pv)
                while len(pend) > LOOKAHEAD:
                    pend.pop(0)()

            blocks = [(kb * 512, 512, True) for kb in range(nfull)] + [(kq0, TW, False)]
            segs = []
            for bi, (k0, kw, full) in enumerate(blocks):
                for hl in range(2):
                    us = []
                    if full:
                        for j in range(4):
                            us.append((j, 128, 0, TW, (k0 // 128) + j))
                    else:
                        for j in range(nsb_d):
                            blk = (kq0 + j * 128) // 128
                            if j * 128 + 64 < TW:
                                us.append((j, 128, j * 128 + 64, TW, blk))
                            us.append((j, 64, j * 128, j * 128 + 64, blk))
                    segs.append((bi, k0, kw, hl, us))
            vas, krTs, kts = {}, {}, {}
            vas[0] = build_kv(blocks[0][0], blocks[0][1], hp)
            krTs[0] = build_krT(blocks[0][0], blocks[0][1])
            kts[0] = build_kt(blocks[0][0], blocks[0][1], 2 * hp + 0, krTs[0])
            yield
            for si, (bi, k0, kw, hl, us) in enumerate(segs):
                nxt = segs[si + 1] if si + 1 < len(segs) else None
                if nxt is not None and nxt[0] != bi:
                    vas[nxt[0]] = build_kv(nxt[1], nxt[2], hp)
                    krTs[nxt[0]] = build_krT(nxt[1], nxt[2])
                for ui, (j, krow, qlo, qhi, blk) in enumerate(us):
                    s_exp_pv(kts[si], vas[bi], hl, j, krow, qlo, qhi, blk)
                    if ui == 0 and nxt is not None:
                        kts[si + 1] = build_kt(nxt[1], nxt[2], 2 * hp + nxt[3], krTs[nxt[0]])
                    yield
            while pend:
                pend.pop(0)()
            ots = [APS() for s in range(NSUB)]
            for hl in range(2):
                ob = ATFn()
                cp(ob[0:65, 0:TW], O[hl][0:65, 0:TW], R=[O[hl]], W=[ob])
                for s in range(NSUB):
                    tr(ots[s][0:SW, hl * 66:(hl + 1) * 66], ob[0:66, s * SW:(s + 1) * SW], ident[0:66, 0:66], R=[ob, cst], W=[ots[s]])
                yield
            for s in range(NSUB):
                pv_ = ots[s][0:SW, 0:132].rearrange("p (h c) -> p h c", h=2)
                rs = ASFn()
                recip(rs[0:SW, 0:2], pv_[:, :, 64], R=[ots[s]], W=[rs])
                tt(otok[0:SW, s, hp * 128:(hp + 1) * 128].rearrange("p (h c) -> p h c", h=2), pv_[:, :, 0:64],
                   rs[0:SW, 0:2].unsqueeze(2).to_broadcast([SW, 2, 64]), ALU.mult, R=[ots[s], rs], W=[otok])
            yield

    def group_norm_tok(l, o, SW, s, kslot, goff, pbank=None):
        jk = TF()
        ms = SF()
        act(jk[0:SW, 0:256], o[0:SW, 0:256], AF.Square, R=[o], W=[jk, ms], scale=1.0 / 16.0, accum=ms[0:SW, 0:1])
        rs = rstd_from(ms[0:SW, 0:1], SW, [ms])
        ob = TB()
        stt(ob[0:SW, 0:256], o[0:SW, 0:256], rs[0:SW, 0:1], rowsL[l][0:SW, goff:goff + 256], ALU.mult, ALU.mult,
            R=[o, rs, rowsL[l]], W=[ob])
        p = pbank if pbank is not None else PSN()
        pv = p[:, :].bitcast(BF16)
        for j in range(2):
            tr(pv[:, j * SW:(j + 1) * SW], ob[0:SW, j * 128:(j + 1) * 128], identb[0:SW, 0:SW], R=[ob, identb], W=[p])
        cp(mixT[:, kslot:kslot + 2, s * SW:(s + 1) * SW], pv[:, 0:2 * SW].rearrange("p (j c) -> p j c", j=2),
           R=[p], W=[ATS[s]])

    def group_norm_fm(l, ybuf, TW, kslot, gcol):
        p = PSN()
        for j in range(2):
            sq = TF()
            act(sq[:, 0:TW], ybuf[:, j, 0:TW], AF.Square, R=[ybuf], W=[sq], eng="act")
            mm(p[:, 0:TW], ones, sq[:, 0:TW], j == 0, j == 1, R=[cst, sq], W=[p])
        r1 = TF()
        act(r1[:, 0:TW], p[:, 0:TW], AF.Ln, R=[p, cst], W=[r1], scale=1.0 / 256.0, bias=epsc)
        r2 = TF()
        act(r2[:, 0:TW], r1[:, 0:TW], AF.Exp, R=[r1], W=[r2], scale=-0.5)
        for j in range(2):
            stt(mixT[:, kslot + j, 0:TW], ybuf[:, j, 0:TW], ppL[l][:, gcol + j:gcol + j + 1], r2[:, 0:TW], ALU.mult, ALU.mult,
                R=[ybuf, ppL[l], r2], W=ATS)

    def conv_fm(inb, c, TW, H, wcol, bcol, l, outap, outbuf, eng="dve"):
        pl = ppL[l]
        inbuf = inb.s(c) if isinstance(inb, BufGroup) else inb
        if bcol is not None:
            ts(outap, inb[:, c, H:H + TW], pl[:, wcol + H:wcol + H + 1], pl[:, bcol:bcol + 1], ALU.mult, ALU.add,
               R=[inbuf, pl], W=[outbuf], eng=eng)
        else:
            ts(outap, inb[:, c, H:H + TW], pl[:, wcol + H:wcol + H + 1], None, ALU.mult, None, R=[inbuf, pl], W=[outbuf], eng=eng)
        for kk in range(H - 1, -1, -1):
            if eng == "dve":
                stt(outap, inb[:, c, kk:kk + TW], pl[:, wcol + kk:wcol + kk + 1], outap, ALU.mult, ALU.add,
                    R=[inbuf, pl, outbuf], W=[outbuf], eng=eng)
            else:
                tmp = TF()
                ts(tmp[:, 0:TW], inb[:, c, kk:kk + TW], pl[:, wcol + kk:wcol + kk + 1], None, ALU.mult, None,
                   R=[inbuf, pl], W=[tmp], eng=eng)
                tt(outap, outap, tmp[:, 0:TW], ALU.add, R=[outbuf, tmp], W=[outbuf], eng=eng)

    def tile(l, g, i):
        TW, SW, NSUB = g["TW"], g["SW"], g["NSUB"]
        n = g["name"]
        pl, rw, dv = ppL[l], rowsL[l], dervL[l]
        xsrc = g["x"] if l == 0 else g["x1"]
        xdst = g["x1"] if l == 0 else g["y"]
        r0 = i * TW

        def load_x(ii, slot):
            rr = ii * TW
            for s_ in range(NSUB):
                dma(xts[slot][0:SW, s_, :], xsrc[rr + s_ * SW:rr + (s_ + 1) * SW, :], XS[s_], W=[XS[s_]])
            dma(ropets[slot][0:SW, 0:NSUB, :], g["rope"][rr:rr + TW, :].rearrange("(s p) d -> p s d", p=SW), ropets[slot],
                W=[ropets[slot]])

        if xstate["pre"] == (l, n, i):
            slot = xstate["slot"]
        else:
            xstate["n"] += 1
            slot = xstate["n"] % 2
            load_x(i, slot)
        xt = xts[slot]
        ropet = ropets[slot]
        if xts[0] is not xts[1] and i + 1 < g["NT"]:
            xstate["pre"] = (l, n, i + 1)
            xstate["slot"] = 1 - slot
            load_x(i + 1, 1 - slot)

        def norm_T(gcol_unused):
            for s in range(NSUB):
                jk = TB()
                ms = SF()
                act(jk[0:SW, :], xt[0:SW, s, :], AF.Square, R=[XS[s]], W=[jk, ms], scale=1.0 / 32.0, accum=ms[0:SW, 0:1])
                rs = rstd_from(ms[0:SW, 0:1], SW, [ms])
                xn = TB()
                ts(xn[0:SW, :], xt[0:SW, s, :], rs[0:SW, 0:1], None, ALU.mult, None, R=[XS[s], rs], W=[xn])
                p = PSN()
                pv = p[:, :].bitcast(BF16)
                for kk in range(8):
                    tr(pv[:, kk * SW:(kk + 1) * SW], xn[0:SW, kk * 128:(kk + 1) * 128], identb[0:SW, 0:SW],
                       R=[xn, identb], W=[p])
                cp(actT[:, :, s * SW:(s + 1) * SW], pv[:, 0:8 * SW].rearrange("p (k c) -> p k c", k=8), R=[p], W=[ATS[s]])

        norm_T(None)
        slotA, wA = load_piece(l, 0, ring[0])
        slotZ, wZ = load_piece(l, 1, ring[1])
        def tokmajor_gen():
            for s in range(NSUB):
                pa = PS[0]
                for kk in range(8):
                    mm(pa[0:SW, 0:420], actT[:, kk, s * SW:(s + 1) * SW], wA[:, kk, :], kk == 0, kk == 7, R=[ATS[s], slotA], W=[pa])
                pz = PS[1]
                for kk in range(8):
                    mm(pz[0:SW, 0:256], actT[:, kk, s * SW:(s + 1) * SW], wZ[:, kk, :], kk == 0, kk == 7, R=[ATS[s], slotZ], W=[pz])
                sigmoid_to(szt[0:SW, s, :], szt, pz[0:SW, 0:256], [pz], SW, 256, mul_ap=pz[0:SW, 0:256], mul_R=[pz])
                d1 = SF()
                tt(d1[0:SW, 0:4], pa[0:SW, 416:420], rw[0:SW, RW["dtb"]:RW["dtb"] + 4], ALU.add, R=[pa, rw], W=[d1])
                d2 = SF()
                act(d2[0:SW, 0:4], d1[0:SW, 0:4], AF.Exp, R=[d1], W=[d2])
                act(dtt[0:SW, s, :], d2[0:SW, 0:4], AF.Ln, R=[d2], W=[dtt], bias=1.0)
                yield
                jk = TF()
                msq = SF()
                act(jk[0:SW, 0:256], pa[0:SW, 0:256], AF.Square, R=[pa], W=[jk, msq], scale=1.0 / 16.0, accum=msq[0:SW, 0:1])
                jk2 = TF()
                msk = SF()
                act(jk2[0:SW, 0:128], pa[0:SW, 256:384], AF.Square, R=[pa], W=[jk2, msk], scale=128.0 ** -0.5,
                    accum=msk[0:SW, 0:1])
                rq = rstd_from(msq[0:SW, 0:1], SW, [msq])
                rk = rstd_from(msk[0:SW, 0:1], SW, [msk])
                stt(ckv_st[0:SW, s, :], pa[0:SW, 256:384], rk[0:SW, 0:1], rw[0:SW, RW["gkv"]:RW["gkv"] + 128], ALU.mult, ALU.mult,
                    R=[pa, rk, rw], W=[ckv_st])
                tt(kr_st[0:SW, s, :], pa[0:SW, 384:416], ropet[0:SW, s, 0:32], ALU.mult, R=[pa, ropet], W=[kr_st])
                rbt = TF()
                tt(rbt[0:SW, 0:16], pa[0:SW, 400:416], ropet[0:SW, s, 32:48], ALU.mult, R=[pa, ropet], W=[rbt])
                tt(rbt[0:SW, 16:32], pa[0:SW, 384:400], ropet[0:SW, s, 48:64], ALU.mult, R=[pa, ropet], W=[rbt])
                tt(kr_st[0:SW, s, :], kr_st[0:SW, s, :], rbt[0:SW, 0:32], ALU.add, R=[kr_st, rbt], W=[kr_st])
                qn = TB()
                ts(qn[0:SW, 0:256], pa[0:SW, 0:256], rq[0:SW, 0:1], None, ALU.mult, None, R=[pa, rq], W=[qn])
                yield
                pq = PS[2]
                pqv = pq[:, :].bitcast(BF16)
                for j in range(2):
                    tr(pqv[:, j * SW:(j + 1) * SW], qn[0:SW, j * 128:(j + 1) * 128], identb[0:SW, 0:SW], R=[qn, identb], W=[pq])
                qnT = TB()
                cp(qnT[:, 0:2 * SW], pqv[:, 0:2 * SW], R=[pq], W=[qnT])
                yield
                pq2 = PS[3]
                for j in range(2):
                    mm(pq2[0:SW, 0:384], qnT[:, j * SW:(j + 1) * SW], wqbL[l][:, j, :], j == 0, j == 1, R=[qnT, wqbL[l]], W=[pq2])
                qv = pq2[0:SW, 0:384].rearrange("p (h c) -> p h c", h=4)
                qf = TF()
                qfv = qf[0:SW, 0:384].rearrange("p (h c) -> p h c", h=4)
                cosb = ropet[0:SW, s, 0:32].unsqueeze(1).to_broadcast([SW, 4, 32])
                tt(qfv[:, :, 64:96], qv[:, :, 64:96], cosb, ALU.mult, R=[pq2, ropet], W=[qf])
                q2 = TF()
                q2v = q2[0:SW, 0:128].rearrange("p (h c) -> p h c", h=4)
                tt(q2v[:, :, 0:16], qv[:, :, 80:96], ropet[0:SW, s, 32:48].unsqueeze(1).to_broadcast([SW, 4, 16]), ALU.mult,
                   R=[pq2, ropet], W=[q2])
                tt(q2v[:, :, 16:32], qv[:, :, 64:80], ropet[0:SW, s, 48:64].unsqueeze(1).to_broadcast([SW, 4, 16]), ALU.mult,
                   R=[pq2, ropet], W=[q2])
                tt(qfv[:, :, 64:96], qfv[:, :, 64:96], q2v, ALU.add, R=[qf, q2], W=[qf])
                cp(qfv[:, :, 0:64], qv[:, :, 0:64], R=[pq2], W=[qf])
                sq = TF()
                tt(sq[0:SW, 0:384], qf[0:SW, 0:384], qf[0:SW, 0:384], ALU.mult, R=[qf], W=[sq])
                s4 = SF()
                red(s4[0:SW, 0:4], sq[0:SW, 0:384].rearrange("p (h c) -> p h c", h=4), R=[sq], W=[s4])
                s5 = SF()
                act(s5[0:SW, 0:4], s4[0:SW, 0:4], AF.Ln, R=[s4, cst], W=[s5], scale=1.0 / 96.0, bias=epsc[0:SW, :])
                s6 = SF()
                act(s6[0:SW, 0:4], s5[0:SW, 0:4], AF.Exp, R=[s5], W=[s6], scale=-0.5)
                q3 = TF()
                tt(q3[0:SW, 0:384].rearrange("p (h c) -> p h c", h=4), qfv, s6[0:SW, 0:4].unsqueeze(2).to_broadcast([SW, 4, 96]),
                   ALU.mult, R=[qf, s6], W=[q3])
                qb = TB()
                tt(qb[0:SW, 0:384], q3[0:SW, 0:384], gqkL[l][0:SW, :], ALU.mult, R=[q3, gqkL[l]], W=[qb])
                yield
                pt_ = PS[1]
                ptv = pt_[:, :].bitcast(BF16)
                for h in range(4):
                    tr(ptv[0:96, h * SW:(h + 1) * SW], qb[0:SW, h * 96:(h + 1) * 96], identb[0:SW, 0:SW], R=[qb, identb], W=[pt_])
                cp(QT[:, :, s * SW:(s + 1) * SW], ptv[0:96, 0:4 * SW].rearrange("p (h c) -> p h c", h=4), R=[pt_], W=[QT])
                yield
                make_keys(l, SW, ckv_st[0:SW, s, :], kr_st[0:SW, s, :], [ckv_st, kr_st], g["PAST"] + r0 + s * SW, banks=(PS[2], PS[3]))
                yield

        def fm_gen():
            for pi in range(4):
                slot, wv = load_piece(l, 2 + pi, ring[2])
                for cc in range(4):
                    m = pi * 4 + cc
                    p = PS[4 + m % 4]
                    for kk in range(8):
                        mm(p[:, 0:TW], wv[:, kk, cc * 128:(cc + 1) * 128], actT[:, kk, 0:TW], kk == 0, kk == 7, R=[slot] + ATS, W=[p])
                    if m < 6:
                        act(xbc_in[:, m, 3:3 + TW], p[:, 0:TW], AF.Copy, R=[p], W=[xbc_in.s(m)])
                    elif m < 8:
                        act(lrux_in[:, m - 6, 3:3 + TW], p[:, 0:TW], AF.Copy, R=[p], W=[lrux_in.s(m - 6)])
                    elif m < 10:
                        act(lrug[:, m - 8, 0:TW], p[:, 0:TW], AF.Copy, R=[p], W=[lrug.s(m - 8)])
                    elif m < 12:
                        act(scb[:, m - 10, 0:TW], p[:, 0:TW], AF.Copy, R=[p], W=[scb.s(m - 10)])
                    elif m < 14:
                        act(prod[:, m - 12, 2:2 + TW], p[:, 0:TW], AF.Copy, R=[p], W=[prod.s(m - 12)])
                    else:
                        tt(prod[:, m - 14, 2:2 + TW], prod[:, m - 14, 2:2 + TW], p[:, 0:TW], ALU.mult, R=[prod.s(m - 14), p], W=[prod.s(m - 14)])
                    yield


        gt, gf = tokmajor_gen(), fm_gen()
        t_done = f_done = False
        acc2 = 0.0
        while not (t_done and f_done):
            if not t_done:
                try:
                    next(gt)
                except StopIteration:
                    t_done = True
            acc2 += 0.6 if not t_done else 1.0
            while acc2 >= 1.0 and not f_done:
                acc2 -= 1.0
                try:
                    next(gf)
                except StopIteration:
                    f_done = True
            if f_done:
                acc2 = 0.0
        dma(g["ckv_o"][l, r0:r0 + TW, :].rearrange("(s p) d -> p s d", p=SW), ckv_st[0:SW, 0:NSUB, :], ckv_st, R=[ckv_st])
        dma(g["kr_o"][l, r0:r0 + TW, :].rearrange("(s p) d -> p s d", p=SW), kr_st[0:SW, 0:NSUB, :], kr_st, R=[kr_st])


        def mixers_gen():
            PA, PB = PS[6], PS[7]

            for c in range(6):
                cv = TF()
                conv_fm(xbc_in, c, TW, 3, PP["cws"] + 4 * c, PP["cbs"] + c, l, cv[:, 0:TW], cv, eng="dve")
                sigmoid_to(xbc_c[:, c, 0:TW], xbc_c.s(c), cv[:, 0:TW], [cv], 128, TW, mul_ap=cv[:, 0:TW], mul_R=[cv])
                cp(xbc_in[:, c, 0:3], xbc_in[:, c, TW:TW + 3], R=[xbc_in.s(c)], W=[xbc_in.s(c)], eng="pool")
                yield
            nch = SW // 64
            for s in range(NSUB):
                c0 = s * SW
                p = PA
                pv = p[:, :].bitcast(BF16)
                for c in range(4):
                    tr(pv[0:SW, c * 128:(c + 1) * 128], xbc_c[:, c, c0:c0 + SW], identb[:, :], R=[xbc_c.s(c), identb], W=[p])
                tok = TB()
                cp(tok[0:SW, 0:512], pv[0:SW, 0:512], R=[p], W=[tok])
                dt = dtt[0:SW, s, :]
                dta = SF()
                tt(dta[0:SW, 0:4], dt, dv[0:SW, 4:8], ALU.mult, R=[dtt, dv], W=[dta])
                pc = PB
                mm(pc[0:SW, 0:4], T2[0:SW, 0:SW], dta[0:SW, 0:4], True, True, R=[cst, dta], W=[pc])
                mm(pc[0:SW, 4:8], O2[0:SW, 0:SW], dta[0:SW, 0:4], True, True, R=[cst, dta], W=[pc])
                for c in range(nch):
                    mm(pc[:, 8 + 4 * c:12 + 4 * c], cst[0:SW, 640 + 128 * c:768 + 128 * c], dta[0:SW, 0:4], True, True,
                       R=[cst, dta], W=[pc])
                cum = SF()
                cp(cum[0:SW, 0:8], pc[0:SW, 0:8], R=[pc], W=[cum])
                yield
                ecum = SF()
                act(ecum[0:SW, 0:4], cum[0:SW, 0:4], AF.Exp, R=[cum], W=[ecum])
                wd = SF()
                tt(wd[0:SW, 0:4], cum[0:SW, 4:8], cum[0:SW, 0:4], ALU.subtract, R=[cum], W=[wd])
                we = SF()
                act(we[0:SW, 0:4], wd[0:SW, 0:4], AF.Exp, R=[wd], W=[we])
                wend = SF()
                tt(wend[0:SW, 0:4], we[0:SW, 0:4], dt, ALU.mult, R=[we, dtt], W=[wend])
                dec = SF()
                act(dec[:, 0:4 * nch], pc[:, 8:8 + 4 * nch], AF.Exp, R=[pc], W=[dec])
                yield
                pseg = PA
                for h in range(4):
                    a2 = TF()
                    ts(a2[0:SW, 0:SW], U2[0:SW, 0:SW], dta[0:SW, h:h + 1], None, ALU.mult, None, R=[cst, dta], W=[a2])
                    mm(pseg[0:SW, h * 64:(h + 1) * 64], a2[0:SW, 0:SW], tri2[0:SW, :], True, True, R=[a2, cst], W=[pseg])
                lex = TF()
                act(lex[0:SW, 0:256], pseg[0:SW, 0:256], AF.Exp, R=[pseg], W=[lex])
                dm = TF()
                tt(dm[0:SW, 0:256].rearrange("p (h t) -> p h t", h=4), mask01[0:SW, :].unsqueeze(1).to_broadcast([SW, 4, 64]),
                   dt.unsqueeze(2).to_broadcast([SW, 4, 64]), ALU.mult, R=[cst, dtt], W=[dm])
                tt(lex[0:SW, 0:256], lex[0:SW, 0:256], dm[0:SW, 0:256], ALU.mult, R=[lex, dm], W=[lex])
                yield
                psc = PB
                for c in range(nch):
                    for gg in range(2):
                        mm(psc[c * 64:(c + 1) * 64, gg * 64:(gg + 1) * 64], xbc_c[:, 2 + gg, c0 + c * 64:c0 + (c + 1) * 64],
                           xbc_c[:, 4 + gg, c0 + c * 64:c0 + (c + 1) * 64], True, True, R=[xbc_c.s(2 + gg), xbc_c.s(4 + gg)], W=[psc])
                mb = TB()
                tt(mb[0:SW, 0:256].rearrange("p (g r t) -> p g r t", g=2, r=2),
                   lex[0:SW, 0:256].rearrange("p (g r t) -> p g r t", g=2, r=2),
                   psc[0:SW, 0:128].rearrange("p (g t) -> p g t", g=2).unsqueeze(2).to_broadcast([SW, 2, 2, 64]),
                   ALU.mult, R=[lex, psc], W=[mb])
                xw = TB()
                tt(xw[0:SW, 0:256].rearrange("p (h c) -> p h c", h=4), tok[0:SW, 0:256].rearrange("p (h c) -> p h c", h=4),
                   wend[0:SW, 0:4].unsqueeze(2).to_broadcast([SW, 4, 64]), ALU.mult, R=[tok, wend], W=[xw])
                yield
                py = PA
                for c in range(nch):
                    rs_ = slice(c * 64, (c + 1) * 64)
                    for h in range(4):
                        mm(py[rs_, h * 64:(h + 1) * 64], mb[rs_, h * 64:(h + 1) * 64], tok[rs_, h * 64:(h + 1) * 64], True, True,
                           R=[mb, tok], W=[py])
                    for gg in range(2):
                        mm(py[rs_, 256 + gg * 128:256 + (gg + 1) * 128], xbc_c[:, 4 + gg, c0 + c * 64:c0 + (c + 1) * 64],
                           hTb[:, gg * 128:(gg + 1) * 128], True, True, R=[xbc_c.s(4 + gg), hTb], W=[py])
                    pst = PB
                    for gg in range(2):
                        mm(pst[:, gg * 128:(gg + 1) * 128], tok[rs_, 256 + gg * 128:256 + (gg + 1) * 128],
                           xw[rs_, gg * 128:(gg + 1) * 128], True, True, R=[tok, xw], W=[pst])
                    tt(hT[:, :].rearrange("p (h c) -> p h c", h=4), hT[:, :].rearrange("p (h c) -> p h c", h=4),
                       dec[:, 4 * c:4 * c + 4].unsqueeze(2).to_broadcast([128, 4, 64]), ALU.mult, R=[hT, dec], W=[hT])
                    tt(hT[:, :], hT[:, :], pst[:, 0:256], ALU.add, R=[hT, pst], W=[hT])
                    cp(hTb[:, :], hT[:, :], R=[hT], W=[hTb], eng="pool")
                y1 = TF()
                tt(y1[0:SW, 0:256].rearrange("p (h c) -> p h c", h=4), py[0:SW, 256:512].rearrange("p (h c) -> p h c", h=4),
                   ecum[0:SW, 0:4].unsqueeze(2).to_broadcast([SW, 4, 64]), ALU.mult, R=[py, ecum], W=[y1])
                tt(y1[0:SW, 0:256], y1[0:SW, 0:256], py[0:SW, 0:256], ALU.add, R=[y1, py], W=[y1])
                y2 = TF()
                tt(y2[0:SW, 0:256].rearrange("p (h c) -> p h c", h=4), tok[0:SW, 0:256].rearrange("p (h c) -> p h c", h=4),
                   rw[0:SW, RW["dd"]:RW["dd"] + 4].unsqueeze(2).to_broadcast([SW, 4, 64]), ALU.mult, R=[tok, rw], W=[y2])
                tt(y2[0:SW, 0:256], y2[0:SW, 0:256], y1[0:SW, 0:256], ALU.add, R=[y2, y1], W=[y2])
                tt(y2[0:SW, 0:256], y2[0:SW, 0:256], szt[0:SW, s, :], ALU.mult, R=[y2, szt], W=[y2])
                group_norm_tok(l, y2, SW, s, 2, RW["gob"], pbank=PB)
                yield

            for j in range(2):
                xc = TF()
                conv_fm(lrux_in, j, TW, 3, PP["lcw"] + 4 * j, PP["lcb"] + j, l, xc[:, 0:TW], xc)
                cp(lrux_in[:, j, 0:3], lrux_in[:, j, TW:TW + 3], R=[lrux_in.s(j)], W=[lrux_in.s(j)], eng="pool")
                xcb = TB()
                cp(xcb[:, 0:TW], xc[:, 0:TW], R=[xc], W=[xcb], eng="pool")
                yield
                pr = PA
                mm(pr[:, 0:TW], bdL[l][:, j, :], xcb[:, 0:TW], True, True, R=[bdL[l], xcb], W=[pr])
                pi_ = PB
                mm(pi_[:, 0:TW], bdL[l][:, 2 + j, :], xcb[:, 0:TW], True, True, R=[bdL[l], xcb], W=[pi_])
                r = TF()
                sigmoid_to(r[:, 0:TW], r, pr[:, 0:TW], [pr, dv], 128, TW, nbias=dv[:, 8 + j:9 + j])
                ig = TF()
                sigmoid_to(ig[:, 0:TW], ig, pi_[:, 0:TW], [pi_, dv], 128, TW, nbias=dv[:, 10 + j:11 + j])
                a2 = TF()
                act(a2[:, 0:TW], r[:, 0:TW], AF.Exp, R=[r, dv], W=[a2], scale=dv[:, 2 + j:3 + j])
                ts(a2[:, 0:TW], a2[:, 0:TW], -1.0, 1.0, ALU.mult, ALU.add, R=[a2], W=[a2])
                ts(a2[:, 0:TW], a2[:, 0:TW], 1e-30, None, ALU.max, None, R=[a2], W=[a2])
                act(a2[:, 0:TW], a2[:, 0:TW], AF.Ln, R=[a2], W=[a2])
                act(a2[:, 0:TW], a2[:, 0:TW], AF.Exp, R=[a2], W=[a2], scale=0.5)
                act(r[:, 0:TW], r[:, 0:TW], AF.Exp, R=[r, dv], W=[r], scale=dv[:, j:j + 1])
                tt(a2[:, 0:TW], a2[:, 0:TW], ig[:, 0:TW], ALU.mult, R=[a2, ig], W=[a2])
                tt(a2[:, 0:TW], a2[:, 0:TW], xc[:, 0:TW], ALU.mult, R=[a2, xc], W=[a2])
                k.op("dve", lambda e, o=ig[:, 0:TW], d0=r[:, 0:TW], d1=a2[:, 0:TW], ini=lruh[:, j:j + 1]:
                     e.tensor_tensor_scan(o, d0, d1, ini, ALU.mult, ALU.add), R=[r, a2, lruh], W=[ig])
                cp(lruh[:, j:j + 1], ig[:, TW - 1:TW], R=[ig], W=[lruh])
                yield
                gsq = TF()
                tt(gsq[:, 0:TW], lrug[:, j, 0:TW], lrug[:, j, 0:TW], ALU.mult, R=[lrug.s(j)], W=[gsq])
                ts(gsq[:, 0:TW], gsq[:, 0:TW], 0.044715, 1.0, ALU.mult, ALU.add, R=[gsq], W=[gsq])
                tt(gsq[:, 0:TW], gsq[:, 0:TW], lrug[:, j, 0:TW], ALU.mult, R=[gsq, lrug.s(j)], W=[gsq])
                sigmoid_to(gsq[:, 0:TW], gsq, gsq[:, 0:TW], [gsq], 128, TW, scale=1.5957691216057308, mul_ap=lrug[:, j, 0:TW], mul_R=[lrug.s(j)])
                tt(lrug[:, j, 0:TW], ig[:, 0:TW], gsq[:, 0:TW], ALU.mult, R=[ig, gsq], W=[lrug.s(j)])
            yield
            p = PA
            for j in range(2):
                sq = TF()
                act(sq[:, 0:TW], lrug[:, j, 0:TW], AF.Square, R=[lrug.s(j)], W=[sq])
                mm(p[:, 0:TW], ones, sq[:, 0:TW], j == 0, j == 1, R=[cst, sq], W=[p])
            r1 = TF()
            act(r1[:, 0:TW], p[:, 0:TW], AF.Ln, R=[p, cst], W=[r1], scale=1.0 / 256.0, bias=epsc)
            r2 = TF()
            act(r2[:, 0:TW], r1[:, 0:TW], AF.Exp, R=[r1], W=[r2], scale=-0.5)
            for j in range(2):
                stt(mixT[:, 4 + j, 0:TW], lrug[:, j, 0:TW], pl[:, PP["goc"] + j:PP["goc"] + j + 1], r2[:, 0:TW], ALU.mult, ALU.mult,
                    R=[lrug.s(j), pl, r2], W=ATS)

            for j in range(2):
                ysj = TF()
                conv_fm(prod, j, TW, 2, PP["scw"] + 3 * j, None, l, ysj[:, 0:TW], ysj, eng="dve")
                cp(prod[:, j, 0:2], prod[:, j, TW:TW + 2], R=[prod.s(j)], W=[prod.s(j)], eng="pool")
                tt(scb[:, j, 0:TW], ysj[:, 0:TW], scb[:, j, 0:TW], ALU.mult, R=[ysj, scb.s(j)], W=[scb.s(j)])
                yield
            p = PB
            for j in range(2):
                sq = TF()
                act(sq[:, 0:TW], scb[:, j, 0:TW], AF.Square, R=[scb.s(j)], W=[sq])
                mm(p[:, 0:TW], ones, sq[:, 0:TW], j == 0, j == 1, R=[cst, sq], W=[p])
            r1 = TF()
            act(r1[:, 0:TW], p[:, 0:TW], AF.Ln, R=[p, cst], W=[r1], scale=1.0 / 256.0, bias=epsc)
            r2 = TF()
            act(r2[:, 0:TW], r1[:, 0:TW], AF.Exp, R=[r1], W=[r2], scale=-0.5)
            for j in range(2):
                stt(mixT[:, 6 + j, 0:TW], scb[:, j, 0:TW], pl[:, PP["god"] + j:PP["god"] + j + 1], r2[:, 0:TW], ALU.mult, ALU.mult,
                    R=[scb.s(j), pl, r2], W=ATS)


        ga = attention_gen(l, g, i)
        gm = mixers_gen()
        kq0_ = g["PAST"] + i * TW
        ua = 2 * (kq0_ // 512 + 1) * 9 + 8
        um = 6 + NSUB * 6 + 8
        ratio = um / float(ua)
        acc = 0.0
        a_done = m_done = False
        while not (a_done and m_done):
            if not a_done:
                try:
                    next(ga)
                except StopIteration:
                    a_done = True
            acc += ratio if not a_done else 1.0
            while acc >= 1.0 and not m_done:
                acc -= 1.0
                try:
                    next(gm)
                except StopIteration:
                    m_done = True
            if m_done:
                acc = 0.0
        for s in range(NSUB):
            group_norm_tok(l, View(otok, otok[:, s, :]), SW, s, 0, RW["goa"])

        if debug and l == 0 and n == "p" and i == 0:
            dma(dbg_mix[:, :, 0:TW], mixT[:, :, 0:TW], ATS[0], R=ATS)
        wo = [load_piece(l, 6 + hh) for hh in range(2)]
        for s in range(NSUB):
            for hh in range(2):
                slot, wv = wo[hh]
                p = PSN()
                for kk in range(8):
                    mm(p[0:SW, :], mixT[:, kk, s * SW:(s + 1) * SW], wv[:, kk, :], kk == 0, kk == 7, R=[ATS[s], slot], W=[p])
                tt(xt[0:SW, s, hh * 512:(hh + 1) * 512], xt[0:SW, s, hh * 512:(hh + 1) * 512], p[0:SW, :], ALU.add,
                   R=[XS[s], p], W=[XS[s]])

        norm_T(None)
        def ffn_stage2(r, accs):
            for j in range(2):
                act(accs[j][:, 0:TW], accs[j][:, 0:TW], AF.Silu, R=[accs[j]], W=[accs[j]])
            for j in range(2):
                tt(hff[:, 2 * r + j, 0:TW], accs[j][:, 0:TW], accs[2 + j][:, 0:TW], ALU.mult, R=[accs[j], accs[2 + j]],
                   W=hff_bufs(2 * r + j))

        pending = None
        for r in range(11):
            slot, wv = load_piece(l, 8 + r)
            accs = []
            for cc in range(4):
                mi = r * 4 + cc
                p = PSN()
                for kk in range(8):
                    mm(p[:, 0:TW], wv[:, kk, cc * 128:(cc + 1) * 128], actT[:, kk, 0:TW], kk == 0, kk == 7, R=[slot] + ATS, W=[p])
                u = tmpF[mi % 3]
                act(u[:, 2:2 + TW], p[:, 0:TW], AF.Copy, R=[p], W=[u])
                cp(u[:, 0:2], ffn_halo[:, mi, :], R=[ffn_halo], W=[u], eng="pool")
                cp(ffn_halo[:, mi, :], u[:, TW:TW + 2], R=[u], W=[ffn_halo], eng="pool")
                acc = tmpF[3 + mi % 8]
                act(acc[:, 0:TW], p[:, 0:TW], AF.Identity, R=[p, pl], W=[acc], scale=pl[:, PP["fcw"] + 3 * mi + 2:PP["fcw"] + 3 * mi + 3],
                    bias=pl[:, PP["fcb"] + mi:PP["fcb"] + mi + 1])
                stt(acc[:, 0:TW], u[:, 1:1 + TW], pl[:, PP["fcw"] + 3 * mi + 1:PP["fcw"] + 3 * mi + 2], acc[:, 0:TW], ALU.mult, ALU.add,
                    R=[u, pl, acc], W=[acc])
                stt(acc[:, 0:TW], u[:, 0:TW], pl[:, PP["fcw"] + 3 * mi:PP["fcw"] + 3 * mi + 1], acc[:, 0:TW], ALU.mult, ALU.add,
                    R=[u, pl, acc], W=[acc])
                accs.append(acc)
                if cc == 1 and pending is not None:
                    ffn_stage2(*pending)
                    pending = None
            pending = (r, accs)
        ffn_stage2(*pending)
        for hh in range(2):
            pss = [PSN() for s in range(NSUB)]
            for kg, (k0, nk) in enumerate(((0, 8), (8, 8), (16, 6))):
                slot, wv = load_piece(l, 19 + hh * 3 + kg)
                for s in range(NSUB):
                    for kk in range(nk):
                        kf = k0 + kk
                        mm(pss[s][0:SW, :], hff[:, kf, s * SW:(s + 1) * SW], wv[:, kk, :], kf == 0, kf == 21, R=hff_bufs(kf) + [slot], W=[pss[s]])
            for s in range(NSUB):
                tt(xt[0:SW, s, hh * 512:(hh + 1) * 512], xt[0:SW, s, hh * 512:(hh + 1) * 512], pss[s][0:SW, :], ALU.add,
                   R=[XS[s], pss[s]], W=[XS[s]])
                if hh == 1:
                    dma(xdst[r0 + s * SW:r0 + (s + 1) * SW, :], xt[0:SW, s, :], XS[s], R=[XS[s]])

    for l in range(DEPTH):
        prep_params(l)
        k.barrier()
        for g in groups:
            n = g["name"]
            TW, SW = g["TW"], g["SW"]
            dma(xbc_in[:, :, 0:3], g["ssm_conv0"][l], xbc_in, W=[xbc_in])
            dma(hT[:, :].rearrange("p (h c) -> p h c", h=4), g["ssm0"][l], hT, W=[hT])
            dma(lrux_in[:, :, 0:3], g["lru_conv0"][l], lrux_in, W=[lrux_in])
            dma(lruh[:, :], g["lru0"][l], lruh, W=[lruh])
            dma(prod[:, :, 0:2], g["sc_conv0"][l], prod, W=[prod])
            dma(ffn_halo[:, :, :], g["ffn_conv0"][l], ffn_halo, W=[ffn_halo])
            cp(hTb[:, :], hT[:, :], R=[hT], W=[hTb])
            for vi in range(2):
                mset(VA[vi][:, :, :, :], 1.0, W=[VA[vi]])
            if g["PAST"]:
                for b in range(g["PAST"] // 128):
                    cs = TF()
                    dma(cs[:, 0:128], g["ckv_past"][l, b * 128:(b + 1) * 128, :], cs, W=[cs])
                    ks = TF()
                    dma(ks[:, 0:32], g["kr_past"][l, b * 128:(b + 1) * 128, :], ks, W=[ks])
                    make_keys(l, 128, cs[:, 0:128], ks[:, 0:32], [cs, ks], b * 128)
            k.barrier()
            for i in range(g["NT"]):
                tile(l, g, i)
            dma(g["ssm_conv_o"][l], xbc_in[:, :, 0:3], xbc_in, R=[xbc_in])
            dma(g["ssm_o"][l], hT[:, :].rearrange("p (h c) -> p h c", h=4), hT, R=[hT])
            dma(g["lru_conv_o"][l], lrux_in[:, :, 0:3], lrux_in, R=[lrux_in])
            dma(g["lru_o"][l], lruh[:, :], lruh, R=[lruh])
            dma(g["sc_conv_o"][l], prod[:, :, 0:2], prod, R=[prod])
            dma(g["ffn_conv_o"][l], ffn_halo[:, :, :], ffn_halo, R=[ffn_halo])
        k.barrier()
    k.barrier()
    k.emit()
    st.close()
    return nc


_CACHE = {}


def make_in_maps(inp, TP):
    consts = make_consts()
    lay = [prep_layer_params(inp, l) for l in range(DEPTH)]
    rope_p = rope_table(np.arange(TP))
    rope_s = rope_table(PAST + np.arange(DEC_SEQ))
    maps = []
    for c in range(8):
        m = {"consts": consts, "rope_p": rope_p, "rope_s": rope_s}
        for l in range(DEPTH):
            for kk, v in lay[l].items():
                m["%s%d" % (kk, l)] = v
        m["x_p"] = np.ascontiguousarray(inp["x_prompt"][c % BATCH, :TP])
        m["x_s"] = np.ascontiguousarray(inp["x_sample"][c])
        m["ssm_conv0_p"] = np.zeros((DEPTH, 128, 6, 3), np.float32)
        m["ssm0_p"] = np.zeros((DEPTH, 128, 4, 64), np.float32)
        m["lru_conv0_p"] = np.zeros((DEPTH, 128, 2, 3), np.float32)
        m["lru0_p"] = np.zeros((DEPTH, 128, 2), np.float32)
        m["sc_conv0_p"] = np.zeros((DEPTH, 128, 2, 2), np.float32)
        m["ffn_conv0_p"] = np.zeros((DEPTH, 128, NFF, 2), np.float32)
        sc = inp["state_ssm_conv"][:, c]
        m["ssm_conv0_s"] = np.ascontiguousarray(sc.reshape(DEPTH, 3, 6, 128).transpose(0, 3, 2, 1))
        m["ssm0_s"] = np.ascontiguousarray(inp["state_ssm"][:, c].transpose(0, 3, 1, 2))
        lc = inp["state_lru_conv"][:, c]
        m["lru_conv0_s"] = np.ascontiguousarray(lc.reshape(DEPTH, 3, 2, 128).transpose(0, 3, 2, 1))
        m["lru0_s"] = np.ascontiguousarray(inp["state_lru"][:, c].reshape(DEPTH, 2, 128).transpose(0, 2, 1))
        s2 = inp["state_sc_conv"][:, c]
        m["sc_conv0_s"] = np.ascontiguousarray(s2.reshape(DEPTH, 2, 2, 128).transpose(0, 3, 2, 1))
        fc = inp["state_ffn_conv"][:, c][:, :, UP_PERM]
        m["ffn_conv0_s"] = np.ascontiguousarray(fc.reshape(DEPTH, 2, NFF, 128).transpose(0, 3, 2, 1))
        m["ckv_past_s"] = np.ascontiguousarray(inp["cache_mla_ckv"][:, c])
        m["kr_past_s"] = np.ascontiguousarray(inp["cache_mla_krope"][:, c])
        maps.append(m)
    return maps


INV_UP = np.argsort(UP_PERM)


def unpack_group(res, n):
    y = np.stack([r["y_" + n] for r in res])
    ckv = np.stack([r["ckv_o_" + n] for r in res], axis=1)
    kr = np.stack([r["kr_o_" + n] for r in res], axis=1)
    ssm_conv = np.stack([r["ssm_conv_o_" + n].transpose(0, 3, 2, 1).reshape(DEPTH, 3, 768) for r in res], axis=1)
    ssm = np.stack([r["ssm_o_" + n].transpose(0, 2, 3, 1) for r in res], axis=1)
    lru_conv = np.stack([r["lru_conv_o_" + n].transpose(0, 3, 2, 1).reshape(DEPTH, 3, 256) for r in res], axis=1)
    lru = np.stack([r["lru_o_" + n].transpose(0, 2, 1).reshape(DEPTH, 256) for r in res], axis=1)
    sc_conv = np.stack([r["sc_conv_o_" + n].transpose(0, 3, 2, 1).reshape(DEPTH, 2, 256) for r in res], axis=1)
    ffn_conv = np.stack([r["ffn_conv_o_" + n].transpose(0, 3, 2, 1).reshape(DEPTH, 2, 2 * DFF)[:, :, INV_UP] for r in res], axis=1)
    return [np.ascontiguousarray(a.astype(np.float32)) for a in (y, ckv, kr, ssm_conv, ssm, lru_conv, lru, sc_conv, ffn_conv)]


def run(inp, TP=SEQ):
    inp = {kk: np.asarray(v) for kk, v in inp.items()}
    if TP not in _CACHE:
        _CACHE[TP] = build(TP)
    nc = _CACHE[TP]
    maps = make_in_maps(inp, TP)
    res = run_bass_kernel_spmd(nc, maps, core_ids=list(range(8)))
    R = res.results
    p = unpack_group([R[b] for b in range(BATCH)], "p")
    s = unpack_group([R[c] for c in range(DEC_BATCH)], "s")
    return (p[0], s[0], *p[1:], *s[1:])


def kernel(**inputs):
    return run(inputs, SEQ)
```
